# Optimizing a Trainium2 kernel written in Bass

```python
import math
import jax, jax.numpy as jnp
from jax import lax
import numpy as np

D_MODEL = 1024
BATCH = 32
SEQ = 2048
DEPTH = 1

SB_HEADS = 8
SB_HEAD_DIM = 64
SB_WIDTH = SB_HEADS * SB_HEAD_DIM
Q_BLOCK = 128
RW_HEADS = 8
RW_HEAD_DIM = 64
RW_WIDTH = RW_HEADS * RW_HEAD_DIM
DECAY_LORA = 64
ICLR_LORA = 64
GATE_LORA = 128
GN_EPS = 64e-5
D_FF = 4 * D_MODEL
PLE_DIM = 256
NORM_EPS = 1e-6

SB_COLS = 3 * SB_WIDTH
RW_COLS = 3 * RW_WIDTH + DECAY_LORA + ICLR_LORA + GATE_LORA
GATE_COLS = 2 * D_MODEL
IN_COLS = SB_COLS + RW_COLS + GATE_COLS

kernel_name = "hybrid_stickbreak_rwkv7_gated_block"


def rms_norm(x, g):
    xf = x.astype(jnp.float32)
    y = xf * lax.rsqrt(jnp.mean(xf * xf, axis=-1, keepdims=True) + NORM_EPS)
    return (y * g.astype(jnp.float32)).astype(x.dtype)


def stick_breaking_attention(q, k, v):
    q = jnp.transpose(q, (0, 2, 1, 3))
    k = jnp.transpose(k, (0, 2, 1, 3))
    v = jnp.transpose(v, (0, 2, 1, 3))
    seq = q.shape[2]
    scale = 1.0 / math.sqrt(q.shape[-1])
    outs = []
    for blk in range(seq // Q_BLOCK):
        q0 = blk * Q_BLOCK
        kend = q0 + Q_BLOCK
        z = jnp.einsum('bhqd,bhkd->bhqk', q[:, :, q0:kend], k[:, :, :kend]).astype(jnp.float32) * scale
        t_pos = q0 + jnp.arange(Q_BLOCK)[:, None]
        s_pos = jnp.arange(kend)[None, :]
        mask = s_pos < t_pos
        log_om = jnp.where(mask, jax.nn.log_sigmoid(-z), 0.0)
        rev = lax.cumsum(log_om, axis=log_om.ndim - 1, reverse=True)
        log_w = jax.nn.log_sigmoid(z) + rev - log_om
        w = jnp.where(mask, jnp.exp(log_w), 0.0)
        outs.append(jnp.einsum('bhqk,bhkd->bhqd', w.astype(v.dtype), v[:, :, :kend]))
    o = jnp.concatenate(outs, axis=2)
    return jnp.transpose(o, (0, 2, 1, 3))


def rwkv7_recurrence(r, w, k, v, a, b):
    bsz, _, heads, n = r.shape

    def step(state, inp):
        r_t, w_t, k_t, v_t, a_t, b_t = inp
        sa = jnp.einsum('bhvk,bhk->bhv', state, a_t)
        state = state * w_t[:, :, None, :] + sa[..., None] * b_t[:, :, None, :] + v_t[..., None] * k_t[:, :, None, :]
        y = jnp.einsum('bhvk,bhk->bhv', state, r_t)
        return state, y

    state0 = jnp.zeros((bsz, heads, n, n), jnp.float32)
    xs = tuple(jnp.moveaxis(t, 1, 0) for t in (r, w, k, v, a, b))
    _, ys = lax.scan(step, state0, xs)
    return jnp.moveaxis(ys, 0, 1)


def rwkv7_time_mix(u, shift_mu, w0, w2, a0, a2, g2, k_k, k_a, r_k, ln_w, ln_b):
    bsz, seq, _ = u.shape
    prev = jnp.pad(u, ((0, 0), (1, 0), (0, 0)))[:, :-1]
    u = u + (prev - u) * shift_mu
    r, k, v, xw, xa, xg = jnp.split(
        u, [RW_WIDTH, 2 * RW_WIDTH, 3 * RW_WIDTH, 3 * RW_WIDTH + DECAY_LORA,
            3 * RW_WIDTH + DECAY_LORA + ICLR_LORA], axis=-1)
    f32 = jnp.float32
    w_log = -jax.nn.softplus(-(w0 + jnp.tanh(xw) @ w2).astype(f32)) - 0.5
    decay = jnp.exp(-jnp.exp(w_log))
    a = jax.nn.sigmoid((a0 + xa @ a2).astype(f32))
    g = jax.nn.sigmoid(xg) @ g2
    hs = lambda t: t.astype(f32).reshape(bsz, seq, RW_HEADS, RW_HEAD_DIM)
    kk = hs(k * k_k)
    kk = kk / jnp.maximum(jnp.sqrt(jnp.sum(kk * kk, axis=-1, keepdims=True)), 1e-12)
    k_eff = hs(k.astype(f32) * (1.0 + (a - 1.0) * k_a.astype(f32)))
    a_h = a.reshape(bsz, seq, RW_HEADS, RW_HEAD_DIM)
    r_h, v_h = hs(r), hs(v)
    y = rwkv7_recurrence(r_h, decay.reshape(bsz, seq, RW_HEADS, RW_HEAD_DIM), k_eff, v_h, -kk, kk * a_h)
    mu = jnp.mean(y, axis=-1, keepdims=True)
    var = jnp.mean(jnp.square(y - mu), axis=-1, keepdims=True)
    y = ((y - mu) * lax.rsqrt(var + GN_EPS)).reshape(bsz, seq, RW_WIDTH)
    y = y * ln_w.astype(f32) + ln_b.astype(f32)
    bonus = jnp.sum(r_h * k_eff * r_k.astype(f32), axis=-1, keepdims=True) * v_h
    y = y + bonus.reshape(bsz, seq, RW_WIDTH)
    return (y * g.astype(f32)).astype(u.dtype)


def setup_inputs(seed: int = 0) -> dict:
    key = jax.random.key(seed)
    ks = jax.random.split(key, 26)
    nrm = lambda k, shape, fan_in: jax.random.normal(k, shape, jnp.float32) * (fan_in ** -0.5)
    gain = lambda k, shape: 1.0 + 0.02 * jax.random.normal(k, shape, jnp.float32)
    L = DEPTH
    return {
        "x": jax.random.normal(ks[0], (BATCH, SEQ, D_MODEL), jnp.float32),
        "p": jax.random.normal(ks[1], (DEPTH, BATCH, SEQ, PLE_DIM), jnp.float32),
        "attn_norm_g": gain(ks[2], (L, D_MODEL)),
        "w_in": nrm(ks[3], (L, D_MODEL, IN_COLS), D_MODEL),
        "shift_mu": jax.random.uniform(ks[4], (L, RW_COLS), jnp.float32, 0.0, 1.0),
        "decay_w0": jax.random.uniform(ks[5], (L, RW_WIDTH), jnp.float32, -3.0, 1.0),
        "decay_w2": 0.5 * nrm(ks[6], (L, DECAY_LORA, RW_WIDTH), DECAY_LORA),
        "iclr_a0": 0.1 * jax.random.normal(ks[7], (L, RW_WIDTH), jnp.float32),
        "iclr_a2": 0.5 * nrm(ks[8], (L, ICLR_LORA, RW_WIDTH), ICLR_LORA),
        "gate_g2": nrm(ks[9], (L, GATE_LORA, RW_WIDTH), GATE_LORA),
        "k_k": 0.85 + 0.02 * jax.random.normal(ks[10], (L, RW_WIDTH), jnp.float32),
        "k_a": gain(ks[11], (L, RW_WIDTH)),
        "r_k": 0.1 * jax.random.normal(ks[12], (L, RW_HEADS, RW_HEAD_DIM), jnp.float32),
        "ln_x_w": gain(ks[13], (L, RW_WIDTH)),
        "ln_x_b": 0.01 * jax.random.normal(ks[14], (L, RW_WIDTH), jnp.float32),
        "w_up_sb": nrm(ks[15], (L, SB_WIDTH, D_MODEL), SB_WIDTH),
        "w_up_rw": nrm(ks[16], (L, RW_WIDTH, D_MODEL), RW_WIDTH),
        "w_out": nrm(ks[17], (L, D_MODEL, D_MODEL), D_MODEL),
        "mlp_norm_g": gain(ks[18], (L, D_MODEL)),
        "w_ff1": nrm(ks[19], (L, D_MODEL, D_FF), D_MODEL),
        "w_ff2": nrm(ks[20], (L, D_FF, D_MODEL), D_FF),
        "ple_norm_g": gain(ks[21], (L, D_MODEL)),
        "w_ple_gate": nrm(ks[22], (L, D_MODEL, D_MODEL), D_MODEL),
        "w_ple_proj": nrm(ks[23], (L, PLE_DIM, D_MODEL), PLE_DIM),
        "final_norm_g": gain(ks[24], (D_MODEL,)),
    }


def reference(x, p, attn_norm_g, w_in, shift_mu, decay_w0, decay_w2, iclr_a0, iclr_a2, gate_g2,
              k_k, k_a, r_k, ln_x_w, ln_x_b, w_up_sb, w_up_rw, w_out, mlp_norm_g, w_ff1, w_ff2,
              ple_norm_g, w_ple_gate, w_ple_proj, final_norm_g):
    bsz, seq, _ = x.shape
    for i in range(DEPTH):
        h = rms_norm(x, attn_norm_g[i])
        u = h @ w_in[i]
        u_sb, u_rw, u_gate = jnp.split(u, [SB_COLS, SB_COLS + RW_COLS], axis=-1)
        q, k, v = jnp.split(u_sb.reshape(bsz, seq, 3 * SB_HEADS, SB_HEAD_DIM), 3, axis=2)
        o_sb = stick_breaking_attention(q, k, v).reshape(bsz, seq, SB_WIDTH)
        o_rw = rwkv7_time_mix(u_rw, shift_mu[i], decay_w0[i], decay_w2[i], iclr_a0[i], iclr_a2[i],
                              gate_g2[i], k_k[i], k_a[i], r_k[i], ln_x_w[i], ln_x_b[i])
        g_sb, g_rw = jnp.split(jax.nn.sigmoid(u_gate), 2, axis=-1)
        merged = g_sb * (o_sb @ w_up_sb[i]) + g_rw * (o_rw @ w_up_rw[i])
        x = x + merged @ w_out[i]
        h = rms_norm(x, mlp_norm_g[i])
        x = x + jnp.square(jax.nn.relu(h @ w_ff1[i])) @ w_ff2[i]
        gate = jax.nn.sigmoid(rms_norm(x, ple_norm_g[i]) @ w_ple_gate[i])
        x = x + gate * (p[i] @ w_ple_proj[i])
    return rms_norm(x, final_norm_g)
```

```python
import numpy as np
from contextlib import ExitStack
import concourse.bass as bass
import concourse.mybir as mybir
from concourse.bass_utils import run_bass_kernel_spmd

F32 = mybir.dt.float32
BF16 = mybir.dt.bfloat16
AF = mybir.ActivationFunctionType
ALU = mybir.AluOpType

D = 1024
NCORES = 8
SEM_CAP = 16000
NV = 70
(V_GA, V_GM, V_GP, V_MU, V_W0, V_A0, V_KK, V_KA, V_RK, V_LNW, V_LNB) = (0, 8, 16, 24, 38, 42, 46, 50, 54, 58, 62)
V_OMU = 66


class Hd:
    __slots__ = ("name", "lw", "rdc", "rdd")

    def __init__(self, name):
        self.name = name
        self.lw = None
        self.rdc = {}
        self.rdd = []


class Op:
    __slots__ = ("eng", "fn", "deps", "marked", "sem", "val", "is_dma", "n", "epoch", "persist")


class Prog:
    CE = ("pe", "act", "dve", "pool")

    def __init__(self, nc, es):
        self.nc = nc
        self.es = es
        self.engs = dict(pe=nc.tensor, act=nc.scalar, dve=nc.vector, pool=nc.gpsimd, sp=nc.sync)
        self.ops = []
        self.csems = {e: [] for e in self.CE}
        self.ccount = {e: 0 for e in self.CE}
        self.known = {e: {} for e in self.engs}
        self.dsem = {}
        self.last = {e: None for e in self.CE}
        self.dma_live = []
        self.pending = {e: [] for e in self.engs}
        self.epoch = 0
        self.nsem = 0
        self.ninst = 0

    def _newsem(self, name):
        self.nsem += 1
        return self.es.enter_context(self.nc.semaphore(name))

    def _mk(self, eng, fn, reads, writes, is_dma, n=1, key=None):
        o = Op()
        o.eng = eng
        o.fn = fn
        o.marked = False
        o.is_dma = is_dma
        o.n = n
        o.epoch = self.epoch
        o.persist = False
        o.sem = None
        o.val = 0
        deps = {}

        def add(d):
            if d is None or d is o or (d.epoch != self.epoch and not d.persist):
                return
            if (not d.is_dma) and (not is_dma) and d.eng == "pe" and eng == "pe":
                return
            deps[id(d)] = d

        for h in reads:
            add(h.lw)
        for h in writes:
            add(h.lw)
            for r in h.rdc.values():
                add(r)
            for r in h.rdd:
                add(r)
        for d in self.pending[eng]:
            deps[id(d)] = d
        self.pending[eng] = []
        o.deps = list(deps.values())
        for d in o.deps:
            d.marked = True
        for h in writes:
            h.lw = o
            h.rdc = {}
            h.rdd = []
        for h in reads:
            if h.lw is o:
                continue
            if is_dma:
                h.rdd.append(o)
            else:
                h.rdc[eng] = o
        if is_dma:
            if key not in self.dsem:
                self.dsem[key] = [self._newsem("d_" + key), 0]
            ds = self.dsem[key]
            ds[1] += 16 * n
            o.sem = ds[0]
            o.val = ds[1]
            self.dma_live.append(o)
        else:
            self.last[eng] = o
        self.ops.append(o)
        return o

    def op(self, eng, fn, reads=(), writes=()):
        return self._mk(eng, fn, reads, writes, False)

    def dma(self, eng, fn, reads, writes, key, n=1, persist=False):
        o = self._mk(eng, fn, reads, writes, True, n=n, key=key)
        if persist:
            o.persist = True
            self.dma_live.remove(o)
        return o

    def flush(self):
        for o in self.ops:
            if (not o.is_dma) and o.marked:
                self.ccount[o.eng] += 1
                r = self.ccount[o.eng] - 1
                ei = r // SEM_CAP
                while len(self.csems[o.eng]) <= ei:
                    self.csems[o.eng].append(self._newsem("c_%s%d" % (o.eng, len(self.csems[o.eng]))))
                o.sem = self.csems[o.eng][ei]
                o.val = r % SEM_CAP + 1
        for o in self.ops:
            e = self.engs[o.eng]
            need = {}
            for d in o.deps:
                k = id(d.sem)
                if k not in need or need[k][1] < d.val:
                    need[k] = (d.sem, d.val)
            kn = self.known[o.eng]
            for k, (s, v) in need.items():
                if kn.get(k, 0) < v:
                    e.wait_ge(s, v)
                    kn[k] = v
                    self.ninst += 1
            if o.is_dma:
                o.fn(e, o.sem)
            else:
                inst = o.fn(e)
                if o.marked:
                    inst.then_inc(o.sem, 1)
            self.ninst += 1
        self.ops = []

    def barrier(self):
        deps = [o for o in self.last.values() if o is not None and o.epoch == self.epoch]
        deps += self.dma_live
        for d in deps:
            d.marked = True
        self.flush()
        self.epoch += 1
        self.dma_live = []
        for e in self.engs:
            self.pending[e] = list(deps)

    def finish(self):
        self.barrier()
        sp = self.engs["sp"]
        need = {}
        for d in self.pending["sp"]:
            k = id(d.sem)
            if k not in need or need[k][1] < d.val:
                need[k] = (d.sem, d.val)
        for k, (s, v) in need.items():
            sp.wait_ge(s, v)


class Buf:
    def __init__(self, t, name):
        self.t = t
        self.name = name
        self.hd = {}

    def h(self, *key):
        if key not in self.hd:
            self.hd[key] = Hd(self.name + str(key))
        return self.hd[key]

    def hs(self, *ranges):
        import itertools
        return [self.h(*k) for k in itertools.product(*ranges)]

    def __getitem__(self, k):
        return self.t[k]


class _Stop(Exception):
    pass


def build(S, NB, dbg_names=(), stage=None, dbg_sq=0):
    nc = bass.Bass("TRN2", target_bir_lowering=False)
    NT = S // 128
    NTB = S // 512
    HALF = S // 2
    NTH = HALF // 128
    NTBH = HALF // 512
    assert HALF % 512 == 0

    def din(name, shape):
        return nc.dram_tensor(name, list(shape), F32, kind="ExternalInput").ap()

    x_d = din("x", [NB, S, D])
    p_d = din("p", [NB, S, 256])
    x0T_d = din("x0T", [NB, 128, 8])
    wA_in = din("wA_in", [42, 128, 8, 128])
    wB_v = din("wB_v", [128, 8, 512])
    wA_us = din("wA_us", [8, 128, 4, 128])
    wA_ur = din("wA_ur", [8, 128, 4, 128])
    wB_out = din("wB_out", [2, 128, 8, 512])
    wA_ff1 = din("wA_ff1", [32, 128, 8, 128])
    wB_ff2 = din("wB_ff2", [4, 2, 128, 8, 512])
    wB_pg = din("wB_pg", [2, 128, 8, 512])
    wB_pp = din("wB_pp", [2, 128, 2, 512])
    w2a2_d = din("w2a2", [128, 512])
    g2_d = din("g2", [128, 512])
    vecs_d = din("vecs", [128, NV])
    gF_d = din("gF", [128, D])
    out_d = nc.dram_tensor("out", [NB, S, D], F32, kind="ExternalOutput").ap()
    dbg_d = {}

    with ExitStack() as es:
        P = Prog(nc, es)
        cnt = [0]

        def sb(shape, dt, name=None, stack=None):
            cnt[0] += 1
            nm = (name or "t") + "_%d" % cnt[0]
            t = (stack or es).enter_context(nc.sbuf_tensor(nm, list(shape), dt))
            return Buf(t, nm)

        psb = [es.enter_context(nc.psum_tensor("ps%d" % i, [128, 512], F32)) for i in range(8)]
        pbh = [Hd("psb%d" % i) for i in range(8)]
        rot = [0]

        def nb():
            b = rot[0]
            rot[0] = (rot[0] + 1) % 6
            return b

        ident = sb([128, 128], BF16, "ident")
        ones16 = sb([128, 128], BF16, "ones16")
        stri = sb([128, 128], BF16, "stri")
        masks4 = sb([128, 4, 512], BF16, "masks4")
        ones512 = sb([128, 512], BF16, "ones512")
        onesbd = sb([128, 128], F32, "onesbd")
        keepm = sb([128, 512], F32, "keepm")
        MG = sb([128, 128], BF16, "MG")
        MN = sb([64, 64], BF16, "MN")
        vecs = sb([128, NV], F32, "vecs")
        omu = sb([128, 14], F32, "omu")
        oka = sb([128, 4], F32, "oka")
        gF = sb([128, D], F32, "gF")
        w2a2 = sb([128, 512], BF16, "w2a2")
        g2 = sb([128, 512], BF16, "g2")
        hc = Hd("consts")

        h1, h5, hid, hst, hm4, hbd, hkm, hmg, hmn = [Hd("c%d" % i) for i in range(9)]
        P.op("pool", lambda e: e.memset(ones16[:], 1.0), [], [h1])
        P.op("pool", lambda e: e.memset(ones512[:], 1.0), [], [h5])
        P.op("pool", lambda e: e.affine_select(out=ident[:], in_=ones16[:], pattern=[[-1, 128]], compare_op=ALU.is_equal, fill=0.0, base=0, channel_multiplier=1), [h1], [hid])
        P.op("pool", lambda e: e.affine_select(out=stri[:], in_=ones16[:], pattern=[[-1, 128]], compare_op=ALU.is_gt, fill=0.0, base=0, channel_multiplier=1), [h1], [hst])
        for j in range(4):
            P.op("pool", lambda e, j=j: e.affine_select(out=masks4[:, j, :], in_=ones512[:], pattern=[[1, 512]], compare_op=ALU.is_gt, fill=0.0, base=-128 * j, channel_multiplier=-1), [h5], [hm4])
        P.op("pool", lambda e: e.memset(onesbd[:], 0.0), [], [hbd])
        P.op("pool", lambda e: e.memset(onesbd[0:64, 0:64], 1.0), [], [hbd])
        P.op("pool", lambda e: e.memset(onesbd[64:128, 64:128], 1.0), [], [hbd])
        P.op("pool", lambda e: e.memset(keepm[:], 1.0), [], [hkm])
        P.op("pool", lambda e: e.affine_select(out=keepm[:].rearrange("p (c t) -> p c t", t=64), in_=keepm[:].rearrange("p (c t) -> p c t", t=64),
                                               pattern=[[0, 8], [1, 64]], compare_op=ALU.not_equal, fill=0.0, base=0, channel_multiplier=0), [hkm], [hkm])
        for r0 in (0, 64):
            P.op("pool", lambda e, r0=r0: e.affine_select(out=MG[r0:r0 + 64, 0:64], in_=ones16[r0:r0 + 64, 0:64], pattern=[[1, 64]], compare_op=ALU.is_gt, fill=0.0, base=0, channel_multiplier=-1), [h1], [hmg])
            P.op("pool", lambda e, r0=r0: e.affine_select(out=MG[r0:r0 + 64, 64:128], in_=ones16[r0:r0 + 64, 0:64], pattern=[[1, 64]], compare_op=ALU.is_ge, fill=0.0, base=0, channel_multiplier=-1), [h1], [hmg])
        P.op("pool", lambda e: e.affine_select(out=MN[:], in_=ones16[0:64, 0:64], pattern=[[-1, 64]], compare_op=ALU.is_gt, fill=0.0, base=0, channel_multiplier=1), [h1], [hmn])
        hv = Hd("vecs")
        P.dma("sp", lambda e, s: e.dma_start(out=vecs[:], in_=vecs_d).then_inc(s, 16), [], [hv], "vecs")
        P.dma("sp", lambda e, s: e.dma_start(out=gF[:], in_=gF_d).then_inc(s, 16), [], [hv], "gF")
        P.dma("pool", lambda e, s: e.dma_start(out=w2a2[:], in_=w2a2_d).then_inc(s, 16), [], [hv], "w2a2")
        P.dma("pool", lambda e, s: e.dma_start(out=g2[:], in_=g2_d).then_inc(s, 16), [], [hv], "g2")
        P.op("dve", lambda e: e.tensor_scalar(out=omu[:], in0=vecs[:, V_MU:V_MU + 14], scalar1=-1.0, scalar2=1.0, op0=ALU.mult, op1=ALU.add), [hv], [hc])
        P.op("dve", lambda e: e.tensor_scalar(out=oka[:], in0=vecs[:, V_KA:V_KA + 4], scalar1=-1.0, scalar2=1.0, op0=ALU.mult, op1=ALU.add), [hv], [hc])
        P.barrier()

        def scr(name, shape):
            return nc.dram_tensor(name, list(shape), BF16, kind="Internal").ap()
        sA_in = scr("sA_in", [42, 128, 8, 128])
        sB_v = scr("sB_v", [128, 8, 512])
        sA_us = scr("sA_us", [8, 128, 4, 128])
        sA_ur = scr("sA_ur", [8, 128, 4, 128])
        sB_out = scr("sB_out", [2, 128, 8, 512])
        sA_ff1 = scr("sA_ff1", [32, 128, 8, 128])
        sB_ff2 = scr("sB_ff2", [4, 2, 128, 8, 512])
        sB_pg = scr("sB_pg", [2, 128, 8, 512])
        sB_pp = scr("sB_pp", [2, 128, 2, 512])
        grp = {}

        def precast(gname, pairs):
            h_ = Hd("grp_" + gname)
            grp[gname] = h_

            def fn(e, s_, pairs=pairs):
                for d_, sr_ in pairs:
                    e.dma_start(out=d_, in_=sr_).then_inc(s_, 16)
            P.dma("pool", fn, [], [h_], "pc_" + gname, n=len(pairs), persist=True)
        precast("qk", [(sA_in[j], wA_in[j]) for j in range(8)])
        precast("v", [(sB_v, wB_v)])
        precast("lora", [(sA_in[j], wA_in[j]) for j in (24, 25)])
        precast("rkv", [(sA_in[j], wA_in[j]) for j in range(12, 24)])
        precast("gates", [(sA_in[j], wA_in[j]) for j in range(26, 42)])
        precast("us", [(sA_us[j], wA_us[j]) for j in range(8)])
        precast("ur", [(sA_ur[j], wA_ur[j]) for j in range(8)])
        precast("out", [(sB_out[j], wB_out[j]) for j in range(2)])
        for g_ in range(4):
            precast("ff1_%d" % g_, [(sA_ff1[j], wA_ff1[j]) for j in range(g_ * 8, g_ * 8 + 8)])
            precast("ff2_%d" % g_, [(sB_ff2[g_, j], wB_ff2[g_, j]) for j in range(2)])
        precast("pg", [(sB_pg[j], wB_pg[j]) for j in range(2)])
        precast("pp", [(sB_pp[j], wB_pp[j]) for j in range(2)])

        hT = sb([128, 8, S], BF16, "hT")
        QK = sb([128, 8, S], BF16, "QK")
        OO = sb([128, 8, S], BF16, "OO")
        WA = []
        WB = []
        wa_i = [0]
        wb_i = [0]

        def rings(ph_, na, nb_):
            WA[:] = [sb([128, 8, 128], BF16, "WA%d" % i, ph_) for i in range(na)]
            WB[:] = [sb([128, 8, 512], BF16, "WB%d" % i, ph_) for i in range(nb_)]
            wa_i[0] = 0
            wb_i[0] = 0

        def loadA(src, g, nk=8):
            i = wa_i[0]
            wa_i[0] = (i + 1) % len(WA)
            w = WA[i]
            P.dma("sp", lambda e, s, w=w, src=src, nk=nk: e.dma_start(out=w[:, 0:nk, :], in_=src).then_inc(s, 16), [grp[g]], [w.h()], "WA%d" % i)
            return w

        def loadB(src, g, nk=8):
            i = wb_i[0]
            wb_i[0] = (i + 1) % len(WB)
            w = WB[i]
            P.dma("sp", lambda e, s, w=w, src=src, nk=nk: e.dma_start(out=w[:, 0:nk, :], in_=src).then_inc(s, 16), [grp[g]], [w.h()], "WB%d" % i)
            return w

        ev_i = [0]

        def evac_copy(out, in_, reads, writes, eng=None):
            if eng is None:
                eng = ("act", "dve")[ev_i[0] % 2]
                ev_i[0] += 1
            if eng == "act":
                P.op("act", lambda e, out=out, in_=in_: e.copy(out=out, in_=in_), reads, writes)
            else:
                P.op("dve", lambda e, out=out, in_=in_: e.tensor_copy(out=out, in_=in_), reads, writes)

        def tbs(t0, n):
            return range(t0 // 512, (t0 + n - 1) // 512 + 1)

        def norm_transpose(src_ap, src_h, gcol, dstT, tt, L):
            ss = L["ss"][L["i"] % 4]
            L["i"] += 1
            junk = L["junk"]
            hb = L["hb"][L["i"] % 2]
            P.op("act", lambda e: e.activation(out=junk[:], in_=src_ap, func=AF.Square, accum_out=ss[:, 0:1]), [src_h], [junk.h(), ss.h()])
            P.op("act", lambda e: e.activation(out=ss[:, 1:2], in_=ss[:, 0:1], func=AF.Sqrt, scale=1.0 / D, bias=1e-6), [ss.h()], [ss.h()])
            P.op("dve", lambda e: e.reciprocal(out=ss[:, 2:3], in_=ss[:, 1:2]), [ss.h()], [ss.h()])
            P.op("dve", lambda e: e.tensor_scalar(out=hb[:], in0=src_ap, scalar1=ss[:, 2:3], scalar2=None, op0=ALU.mult), [src_h, ss.h()], [hb.h()])
            import os
            if os.environ.get('DBG_NOTR'):
                return
            b = nb()
            pv = psb[b][:].bitcast(BF16)

            def tr(e):
                for kc in range(8):
                    i = e.transpose(pv[:, kc * 128:(kc + 1) * 128], hb[:, kc * 128:(kc + 1) * 128], ident[:])
                return i
            P.op("pe", tr, [hb.h()], [pbh[b]])
            if os.environ.get('DBG_NOEV'):
                return
            for kc in range(8):
                o = dstT[:, kc, tt * 128:(tt + 1) * 128]
                i_ = pv[:, kc * 128:(kc + 1) * 128]
                sc = vecs[:, gcol + kc:gcol + kc + 1]
                wr = [dstT.h(kc, (tt * 128) // 512)]
                if tt % 2 == 0:
                    P.op("act", lambda e, o=o, i_=i_, sc=sc: e.activation(out=o, in_=i_, func=AF.Identity, scale=sc), [pbh[b]], wr)
                else:
                    P.op("dve", lambda e, o=o, i_=i_, sc=sc: e.tensor_scalar(out=o, in0=i_, scalar1=sc, scalar2=None, op0=ALU.mult), [pbh[b]], wr)

        def projA(w, nk, rhsT, t0, kcs_handles):
            b = nb()

            def f(e):
                for kc in range(nk):
                    i = e.matmul(psb[b][:], lhsT=w[:, kc, :], rhs=rhsT[:, kc, t0:t0 + 512], start=(kc == 0), stop=(kc == nk - 1))
                return i
            P.op("pe", f, [w.h()] + kcs_handles, [pbh[b]])
            return b

        def dbg_dump(name, ap, hlist, shape, dt=F32):
            if name not in dbg_names:
                return
            d = nc.dram_tensor("dbg_" + name, list(shape), dt, kind="ExternalOutput").ap()
            dbg_d[name] = d
            P.dma("sp", lambda e, s: e.dma_start(out=d, in_=ap).then_inc(s, 16), hlist, [], "dbg_" + name)

        for sq in range(NB):
            if stage == 'S':
                break
            with ExitStack() as ph:
                L = dict(i=0, ss=[sb([128, 4], F32, "ss", ph) for _ in range(4)], junk=sb([128, D], BF16, "junk", ph),
                         hb=[sb([128, D], BF16, "hb", ph) for _ in range(2)])
                xt = [sb([128, D], F32, "xt", ph) for _ in range(4)]
                for tt in range(NT):
                    xb = xt[tt % 4]
                    P.dma("sp", lambda e, s, xb=xb, tt=tt: e.dma_start(out=xb[:], in_=x_d[sq, tt * 128:(tt + 1) * 128, :]).then_inc(s, 16), [], [xb.h()], "xt%d" % (tt % 4))
                    norm_transpose(xb[:], xb.h(), V_GA, hT, tt, L)
                P.barrier()

            if stage == 'A':
                break
            with ExitStack() as ph:
                rings(ph, 3, 1)
                v16 = sb([128, NT, 512], BF16, "v16", ph)
                hT_all = hT.hs(range(8), range(NTB))
                for j in range(8):
                    w = loadA(sA_in[j], "qk")
                    for tb in range(NTB):
                        b = projA(w, 8, hT, tb * 512, hT.hs(range(8), [tb]))
                        evac_copy(QK[:, j, tb * 512:(tb + 1) * 512], psb[b][:], [pbh[b]], [QK.h(j, tb)])
                w = loadB(sB_v, "v")
                for tt in range(NT):
                    b = nb()

                    def f(e, b=b, tt=tt, w=w):
                        for kc in range(8):
                            i = e.matmul(psb[b][:], lhsT=hT[:, kc, tt * 128:(tt + 1) * 128], rhs=w[:, kc, :], start=(kc == 0), stop=(kc == 7))
                        return i
                    P.op("pe", f, [w.h()] + hT.hs(range(8), [tt // 4]), [pbh[b]])
                    evac_copy(v16[:, tt, :], psb[b][:], [pbh[b]], [v16.h(tt)])

                NW = 6
                e_t = [sb([128, 512], F32, "e_t", ph) for _ in range(NW)]
                sp_t = [sb([128, 512], F32, "sp_t", ph) for _ in range(NW)]
                NL = 8
                NP2 = 6
                L2 = [sb([128, 512], BF16, "L2", ph) for _ in range(NP2)]
                pair_ctr = [0]
                NP4 = 6
                L4 = [sb([128, 512], BF16, "L4", ph) for _ in range(NP4)]
                quad_ctr = [0]
                qz = [[sb([128, 512], BF16, "qz", ph) for _ in range(2)] for _ in range(2)]
                for par_ in range(2):
                    for k_ in range(2):
                        P.op("pool", lambda e, q_=qz[par_][k_]: e.memset(q_[:], 0.0), [], [qz[par_][k_].h()])
                qz_ctr = [0, 0]
                L16 = [sb([128, 512], BF16, "L16", ph) for _ in range(NL)]
                w16 = [sb([128, 512], BF16, "w16", ph) for _ in range(NW)]
                SCALE = 0.125
                tiles = []
                ob_i = 0
                for h in range(8):
                    for sbk in range(NTB):
                        t0 = sbk * 512
                        nk = (t0 + 512) // 128
                        ob = 6 + (ob_i % 2)
                        ob_i += 1
                        pairs = []
                        quads = []
                        single = None
                        qzi = qz_ctr[h % 2] % 2
                        qz_ctr[h % 2] += 1
                        for j, kc in enumerate(range(nk - 1, -1, -1)):
                            t = dict(h=h, par=h % 2, hp=h // 2, sbk=sbk, t0=t0, kc=kc, ob=ob, first=(kc == nk - 1), last=(kc == 0),
                                     diag=(kc * 128 >= t0), jd=kc - t0 // 128, w=len(tiles) % NW, l=len(tiles) % NL,
                                     pairs=list(pairs), quads=list(quads), single=single, mkpair=None, mkquad=None, qzi=qzi)
                            if j % 2 == 1:
                                pidx = pair_ctr[0] % NP2
                                pair_ctr[0] += 1
                                t["mkpair"] = (pidx, single)
                                pairs.append(pidx)
                                single = None
                                if len(pairs) == 2:
                                    qidx = quad_ctr[0] % NP4
                                    quad_ctr[0] += 1
                                    t["mkquad"] = (qidx, pairs[0], pairs[1])
                                    quads.append(qidx)
                                    pairs = []
                            else:
                                single = t["l"]
                            tiles.append(t)

                def stage1(t):
                    pr = slice(64 * t["par"], 64 * t["par"] + 64)
                    et, spt, l16 = e_t[t["w"]], sp_t[t["w"]], L16[t["l"]]
                    zb = nb()
                    t["zb"] = zb
                    kc, t0, hp, jd = t["kc"], t["t0"], t["hp"], t["jd"]
                    c0 = 128 * jd if t["diag"] else 0
                    t["c0"] = c0
                    qzb = qz[t["par"]][t["qzi"]]
                    if t["first"]:
                        P.op("act", lambda e: e.copy(out=qzb[pr, :], in_=QK[pr, hp, t0:t0 + 512]), [QK.h(hp, t["sbk"])], [qzb.h()])
                    P.op("pe", lambda e: e.matmul(psb[zb][:, c0:], lhsT=QK[:, 4 + hp, kc * 128:(kc + 1) * 128], rhs=qzb[:, c0:], start=True, stop=True),
                         [QK.h(4 + hp, kc // 4), qzb.h()], [pbh[zb]])
                    P.op("act", lambda e: e.activation(out=et[:, c0:], in_=psb[zb][:, c0:], func=AF.Exp, scale=-SCALE), [pbh[zb]], [et.h()])
                    P.op("act", lambda e: e.activation(out=spt[:, c0:], in_=et[:, c0:], func=AF.Ln, bias=1.0), [et.h()], [spt.h()])
                    if t["diag"]:
                        P.op("dve", lambda e: e.scalar_tensor_tensor(out=et[:, c0:], in0=psb[zb][:, c0:], scalar=-SCALE, in1=spt[:, c0:], op0=ALU.mult, op1=ALU.subtract),
                             [pbh[zb], spt.h()], [et.h()])
                        if c0 > 0:
                            P.op("pool", lambda e: e.memset(l16[:, 0:c0], 0.0), [], [l16.h()])
                        P.op("pool", lambda e: e.tensor_tensor(out=l16[:, c0:], in0=et[:, c0:], in1=masks4[:, jd, c0:], op=ALU.mult), [et.h()], [l16.h()])
                    else:
                        P.op("dve", lambda e: e.scalar_tensor_tensor(out=l16[:], in0=psb[zb][:], scalar=-SCALE, in1=spt[:], op0=ALU.mult, op1=ALU.subtract),
                             [pbh[zb], spt.h()], [l16.h()])
                    if t["mkpair"] is not None:
                        pidx, sidx = t["mkpair"]
                        l2, lo = L2[pidx], L16[sidx]
                        P.op("pool", lambda e: e.tensor_tensor(out=l2[:], in0=lo[:], in1=l16[:], op=ALU.add), [lo.h(), l16.h()], [l2.h()])
                        if t["mkquad"] is not None:
                            qidx, pa, pb = t["mkquad"]
                            l4, la_, lb_ = L4[qidx], L2[pa], L2[pb]
                            P.op("pool", lambda e: e.tensor_tensor(out=l4[:], in0=la_[:], in1=lb_[:], op=ALU.add), [la_.h(), lb_.h()], [l4.h()])

                def stage2(t):
                    pr = slice(64 * t["par"], 64 * t["par"] + 64)
                    et, spt, l16, w16t = e_t[t["w"]], sp_t[t["w"]], L16[t["l"]], w16[t["w"]]
                    rb = t["zb"]
                    kc, t0, hp, jd, h, ob, first = t["kc"], t["t0"], t["hp"], t["jd"], t["h"], t["ob"], t["first"]
                    prevb = [L4[qi] for qi in t["quads"]] + [L2[pi] for pi in t["pairs"]] + ([L16[t["single"]]] if t["single"] is not None else [])

                    c0 = t["c0"]

                    def f(e):
                        i = e.matmul(psb[rb][:, c0:], lhsT=stri[:], rhs=l16[:, c0:], start=True, stop=(len(prevb) == 0))
                        for j, pb_ in enumerate(prevb):
                            i = e.matmul(psb[rb][:, c0:], lhsT=ones16[:], rhs=pb_[:, c0:], start=False, stop=(j == len(prevb) - 1))
                        return i
                    P.op("pe", f, [l16.h()] + [pb_.h() for pb_ in prevb], [pbh[rb]])

                def stage3(t):
                    et, spt, w16t = e_t[t["w"]], sp_t[t["w"]], w16[t["w"]]
                    rb = t["zb"]
                    jd = t["jd"]
                    c0 = t["c0"]
                    P.op("dve", lambda e: e.tensor_tensor(out=et[:, c0:], in0=psb[rb][:, c0:], in1=spt[:, c0:], op=ALU.subtract), [pbh[rb], spt.h()], [et.h()])
                    P.op("act", lambda e: e.activation(out=w16t[:, c0:], in_=et[:, c0:], func=AF.Exp), [et.h()], [w16t.h()])
                    if t["diag"]:
                        if c0 > 0:
                            P.op("pool", lambda e: e.memset(w16t[:, 0:c0], 0.0), [], [w16t.h()])
                        P.op("pool", lambda e: e.tensor_tensor(out=w16t[:, c0:], in0=w16t[:, c0:], in1=masks4[:, jd, c0:], op=ALU.mult), [w16t.h()], [w16t.h()])

                def stage4(t):
                    pr = slice(64 * t["par"], 64 * t["par"] + 64)
                    w16t = w16[t["w"]]
                    kc, t0, hp, h, ob, first = t["kc"], t["t0"], t["hp"], t["h"], t["ob"], t["first"]
                    P.op("pe", lambda e: e.matmul(psb[ob][:], lhsT=v16[:, kc, hp * 128:(hp + 1) * 128], rhs=w16t[:], start=first, stop=t["last"]),
                         [v16.h(kc), w16t.h()], [pbh[ob]])
                    if t["last"]:
                        evac_copy(OO[pr, hp, t0:t0 + 512], psb[ob][pr, :], [pbh[ob]], [OO.h(hp, t["sbk"])])

                NTL = len(tiles)
                for i in range(NTL + 6):
                    if i < NTL:
                        stage1(tiles[i])
                    if 0 <= i - 3 < NTL:
                        stage2(tiles[i - 3])
                    if 0 <= i - 4 < NTL:
                        stage3(tiles[i - 4])
                    if 0 <= i - 6 < NTL:
                        stage4(tiles[i - 6])
                if sq == dbg_sq:
                    dbg_dump("osb", OO[:, 0:4, :], OO.hs(range(4), range(NTB)), [128, 4, S], BF16)
                    dbg_dump("qk", QK[:, :, :], QK.hs(range(8), range(NTB)), [128, 8, S], BF16)
                    dbg_dump("hT", hT[:, :, :], hT.hs(range(8), range(NTB)), [128, 8, S], BF16)
                P.barrier()

            if stage == 'B':
                break
            with ExitStack() as ph:
                rings(ph, 2, 0)
                TWXA = sb([128, S], BF16, "TWXA", ph)
                SG = sb([128, S], BF16, "SG", ph)
                Ul = [sb([128, 513], F32, "Ul", ph) for _ in range(2)]
                tmp = sb([128, 512], F32, "tmp", ph)
                xs = sb([128, 512], F32, "xs", ph)
                for li, cbw in enumerate((24, 25)):
                    w = loadA(sA_in[cbw], "lora")
                    U = Ul[li]
                    P.op("pool", lambda e, U=U: e.memset(U[:, 0:1], 0.0), [], [U.h()])
                    for tb in range(NTB):
                        b = projA(w, 8, hT, tb * 512, hT.hs(range(8), [tb]))
                        P.op("act", lambda e, U=U, b=b: e.copy(out=U[:, 1:513], in_=psb[b][:]), [pbh[b]], [U.h()])
                        mc = vecs[:, V_MU + 12 + li:V_MU + 13 + li]
                        oc = omu[:, 12 + li:13 + li]
                        P.op("dve", lambda e, U=U, oc=oc: e.tensor_scalar(out=tmp[:], in0=U[:, 1:513], scalar1=oc, scalar2=None, op0=ALU.mult), [U.h()], [tmp.h()])
                        P.op("dve", lambda e, U=U, mc=mc: e.scalar_tensor_tensor(out=xs[:], in0=U[:, 0:512], scalar=mc, in1=tmp[:], op0=ALU.mult, op1=ALU.add), [U.h(), tmp.h()], [xs.h()])
                        P.op("act", lambda e, U=U: e.copy(out=U[:, 0:1], in_=U[:, 512:513]), [U.h(), xs.h()], [U.h()])
                        sl = slice(tb * 512, (tb + 1) * 512)
                        if li == 0:
                            P.op("act", lambda e, sl=sl: e.activation(out=TWXA[0:64, sl], in_=xs[0:64, :], func=AF.Tanh), [xs.h()], [TWXA.h(tb, 0)])
                            P.op("act", lambda e, sl=sl: e.copy(out=TWXA[64:128, sl], in_=xs[64:128, :]), [xs.h()], [TWXA.h(tb, 1)])
                        else:
                            P.op("act", lambda e, sl=sl: e.activation(out=SG[:, sl], in_=xs[:], func=AF.Sigmoid), [xs.h()], [SG.h(tb)])

                carve_i = [0]
                per_row = S // 1024

                def carve512(name):
                    i = carve_i[0]
                    if i >= 8 * per_row:
                        return sb([128, 512], F32, name, ph)
                    carve_i[0] += 1
                    k, half = divmod(i, per_row)
                    ap = QK.t[:, k, half * 1024:(half + 1) * 1024].bitcast(F32)
                    return Buf(ap, "cv_%s" % name)
                Wrkv = sb([128, 3, 8, 128], BF16, "Wrkv", ph)
                W32 = sb([128, 8, 128], F32, "W32", ph)
                ones32 = sb([128, 128], F32, "ones32", ph)
                x0 = sb([128, 8], F32, "x0", ph)
                t0s = sb([128, 8], F32, "t0s", ph)
                t0b = sb([128, 64], F32, "t0b", ph)
                h0 = sb([128, 8, 64], F32, "h0", ph)
                P.op("pool", lambda e: e.memset(ones32[:], 1.0), [], [ones32.h()])
                P.dma("sp", lambda e, s_: e.dma_start(out=x0[:], in_=x0T_d[sq]).then_inc(s_, 16), [], [x0.h()], "x0")
                P.op("dve", lambda e: e.tensor_tensor(out=t0s[:], in0=x0[:], in1=x0[:], op=ALU.mult), [x0.h()], [t0s.h()])
                P.op("dve", lambda e: e.reduce_sum(out=t0s[:, 0:1], in_=t0s[:], axis=mybir.AxisListType.X), [t0s.h()], [t0s.h()])
                P.op("dve", lambda e: e.tensor_copy(out=t0b[:], in_=t0s[:, 0:1].broadcast_to([128, 64])), [t0s.h()], [t0b.h()])
                b0_ = nb()
                P.op("pe", lambda e, b0_=b0_: e.matmul(psb[b0_][:, 0:64], lhsT=ones32[:], rhs=t0b[:], start=True, stop=True), [ones32.h(), t0b.h()], [pbh[b0_]])
                P.op("act", lambda e, b0_=b0_: e.activation(out=t0s[:, 2:4], in_=psb[b0_][:, 0:2], func=AF.Sqrt, scale=1.0 / D, bias=1e-6), [pbh[b0_]], [t0s.h()])
                P.op("dve", lambda e: e.reciprocal(out=t0s[:, 4:6], in_=t0s[:, 2:4]), [t0s.h()], [t0s.h()])
                P.op("dve", lambda e: e.scalar_tensor_tensor(out=x0[:], in0=x0[:], scalar=t0s[:, 4:5], in1=vecs[:, V_GA:V_GA + 8], op0=ALU.mult, op1=ALU.mult), [x0.h(), t0s.h()], [x0.h()])
                P.op("dve", lambda e: e.tensor_copy(out=h0[:], in_=x0[:].unsqueeze(2).broadcast_to([128, 8, 64])), [x0.h()], [h0.h()])
                Ur = [sb([128, 513], F32, "Ur", ph) for _ in range(3)]
                Xr = [sb([128, 512], F32, "Xr", ph) for _ in range(3)]
                lw = carve512("lw")
                At = carve512("At")
                Gt = carve512("Gt")
                kk = carve512("kk")
                kk2 = carve512("kk2")
                sd = carve512("sd")
                kkn = carve512("kkn")
                keff = carve512("keff")
                bvec = carve512("bvec")
                clw = carve512("clw")
                E1 = carve512("E1")
                E2 = carve512("E2")
                E3 = carve512("E3")
                dd = carve512("dd")
                rk = sb([128, 512], F32, "rk", ph)
                WC = sb([128, 8], F32, "WC", ph)
                AR = sb([128, 8, 128], BF16, "AR", ph)
                BK = sb([128, 8, 128], BF16, "BK", ph)
                AR32 = sb([128, 8, 128], F32, "AR32", ph)
                BK32 = sb([128, 8, 128], F32, "BK32", ph)
                BH = sb([128, 8, 64], BF16, "BH", ph)
                KH = sb([128, 8, 64], BF16, "KH", ph)
                V16 = sb([128, 512], BF16, "V16", ph)
                TOKB = sb([128, 8, 128], BF16, "TOKB", ph)
                Gs = sb([128, 2, 8, 128], BF16, "Gs", ph)
                NnA = [sb([64, 8, 64], BF16, "Nn", ph) for _ in range(2)]
                PpA = [sb([64, 8, 64], BF16, "Pp", ph) for _ in range(2)]
                XxA = [sb([64, 8, 64], BF16, "Xx", ph) for _ in range(2)]
                NnB = [sb([64, 8, 64], BF16, "Nn", ph) for _ in range(2)]
                PpB = [sb([64, 8, 64], BF16, "Pp", ph) for _ in range(2)]
                XxB = [sb([64, 8, 64], BF16, "Xx", ph) for _ in range(2)]
                TT = sb([64, 2, 8, 64], BF16, "TT", ph)
                S32 = sb([128, 64], F32, "S32", ph)
                S16z = [sb([128, 2, 64], BF16, "S16z", ph) for _ in range(2)]
                sz_i = [0]
                BDm = sb([128, 2, 64], F32, "BDm", ph)
                P.op("pool", lambda e: e.memset(BDm[:], 0.0), [], [BDm.h()])
                P.op("pool", lambda e: e.memset(BDm[0:64, 0, :], 1.0), [], [BDm.h()])
                P.op("pool", lambda e: e.memset(BDm[64:128, 1, :], 1.0), [], [BDm.h()])
                ATOK = sb([64, 8, 128], BF16, "ATOK", ph)
                Qs = sb([64, 2, 8, 64], BF16, "Qs", ph)
                W00s = sb([64, 2, 8, 64], BF16, "W00s", ph)
                MT = sb([128, 8, 64], BF16, "MT", ph)
                RP = sb([128, 8, 64], BF16, "RP", ph)
                V0 = sb([128, 8, 128], BF16, "V0", ph)
                UV = sb([128, 8, 128], BF16, "UV", ph)
                Yt, Y2, mt, msq, var, yc = kk, kk2, sd, dd, E2, E3

                def v3(t):
                    return t[:].rearrange("p (c t) -> p c t", t=64)

                for hp in range(4):
                    for idx in range(3):
                        cbw = 12 + idx * 4 + hp
                        P.dma("sp", lambda e, s, idx=idx, cbw=cbw: e.dma_start(out=Wrkv[:, idx, :, :], in_=sA_in[cbw]).then_inc(s, 16), [grp["rkv"]], [Wrkv.h(idx)], "Wrkv%d" % idx)
                        P.op("pool", lambda e, idx=idx: e.memset(Ur[idx][:, 0:1], 0.0), [], [Ur[idx].h()])
                    P.op("pool", lambda e: e.memset(S32[:], 0.0), [], [S32.h()])
                    P.op("pool", lambda e, z_=S16z[sz_i[0] % 2]: e.memset(z_[:], 0.0), [], [S16z[sz_i[0] % 2].h()])
                    if hp == 0:
                        P.op("pool", lambda e: e.memset(V0[0:64].rearrange("p c k -> p (c k)"), 0.0), [], [V0.h()])
                    def rw_block(hp, tb):
                        T0 = tb * 512
                        hTh = hT.hs(range(8), [tb])
                        for idx in range(3):
                            b = nb()

                            def f(e, b=b, idx=idx, T0=T0):
                                for kc in range(8):
                                    i = e.matmul(psb[b][:], lhsT=Wrkv[:, idx, kc, :], rhs=hT[:, kc, T0:T0 + 512], start=(kc == 0), stop=(kc == 7))
                                return i
                            P.op("pe", f, [Wrkv.h(idx)] + hTh, [pbh[b]])
                            U = Ur[idx]
                            X = Xr[idx]
                            ci = idx * 4 + hp
                            mc = vecs[:, V_MU + ci:V_MU + ci + 1]
                            oc = omu[:, ci:ci + 1]
                            P.op("act", lambda e, U=U, b=b: e.copy(out=U[:, 1:513], in_=psb[b][:]), [pbh[b]], [U.h()])
                            if tb == 0 and idx < 2:
                                cbw0 = 12 + idx * 4 + hp
                                P.dma("sp", lambda e, s_, cbw0=cbw0: e.dma_start(out=W32[:], in_=wA_in[cbw0]).then_inc(s_, 16), [], [W32.h()], "W32")
                                b2 = nb()

                                def f0(e, b2=b2):
                                    for kc in range(8):
                                        i = e.matmul(psb[b2][:, 0:64], lhsT=W32[:, kc, :], rhs=h0[:, kc, :], start=(kc == 0), stop=(kc == 7))
                                    return i
                                P.op("pe", f0, [W32.h(), h0.h()], [pbh[b2]])
                                P.op("act", lambda e, U=U, b2=b2: e.copy(out=U[:, 1:2], in_=psb[b2][:, 0:1]), [pbh[b2]], [U.h()])
                            P.op("dve", lambda e, U=U, oc=oc: e.tensor_scalar(out=tmp[:], in0=U[:, 1:513], scalar1=oc, scalar2=None, op0=ALU.mult), [U.h()], [tmp.h()])
                            P.op("dve", lambda e, U=U, mc=mc, X=X: e.scalar_tensor_tensor(out=X[:], in0=U[:, 0:512], scalar=mc, in1=tmp[:], op0=ALU.mult, op1=ALU.add), [U.h(), tmp.h()], [X.h()])
                            P.op("act", lambda e, U=U: e.copy(out=U[:, 0:1], in_=U[:, 512:513]), [U.h(), X.h()], [U.h()])
                        R, K, V = Xr
                        cs = slice(hp * 128, (hp + 1) * 128)
                        ts = slice(T0, T0 + 512)
                        b = nb()
                        P.op("pe", lambda e, b=b, cs=cs, ts=ts: e.matmul(psb[b][:], lhsT=w2a2[0:64, cs], rhs=TWXA[0:64, ts], start=True, stop=True), [TWXA.h(tb, 0)], [pbh[b]])
                        P.op("act", lambda e, b=b: e.activation(out=lw[:], in_=psb[b][:], func=AF.Sigmoid, bias=vecs[:, V_W0 + hp:V_W0 + hp + 1]), [pbh[b]], [lw.h()])
                        P.op("dve", lambda e: e.tensor_scalar(out=lw[:], in0=lw[:], scalar1=-0.6065306597126334, scalar2=None, op0=ALU.mult), [lw.h()], [lw.h()])
                        b = nb()
                        P.op("pe", lambda e, b=b, cs=cs, ts=ts: e.matmul(psb[b][:], lhsT=w2a2[64:128, cs], rhs=TWXA[64:128, ts], start=True, stop=True), [TWXA.h(tb, 1)], [pbh[b]])
                        P.op("act", lambda e, b=b: e.activation(out=At[:], in_=psb[b][:], func=AF.Sigmoid, bias=vecs[:, V_A0 + hp:V_A0 + hp + 1]), [pbh[b]], [At.h()])
                        b = nb()
                        P.op("pe", lambda e, b=b, cs=cs, ts=ts: e.matmul(psb[b][:], lhsT=g2[:, cs], rhs=SG[:, ts], start=True, stop=True), [SG.h(tb)], [pbh[b]])
                        P.op("act", lambda e, b=b: e.copy(out=Gt[:], in_=psb[b][:]), [pbh[b]], [Gt.h()])
                        P.op("dve", lambda e: e.tensor_scalar(out=kk[:], in0=K[:], scalar1=vecs[:, V_KK + hp:V_KK + hp + 1], scalar2=None, op0=ALU.mult), [K.h()], [kk.h()])
                        P.op("dve", lambda e: e.tensor_tensor(out=kk2[:], in0=kk[:], in1=kk[:], op=ALU.mult), [kk.h()], [kk2.h()])
                        b = nb()
                        P.op("pe", lambda e, b=b: e.matmul(psb[b][:], lhsT=onesbd[:], rhs=kk2[:], start=True, stop=True), [kk2.h()], [pbh[b]])
                        P.op("act", lambda e, b=b: e.activation(out=sd[:], in_=psb[b][:], func=AF.Sqrt), [pbh[b]], [sd.h()])
                        P.op("dve", lambda e: e.tensor_scalar(out=tmp[:], in0=At[:], scalar1=vecs[:, V_KA + hp:V_KA + hp + 1], scalar2=oka[:, hp:hp + 1], op0=ALU.mult, op1=ALU.add), [At.h()], [tmp.h()])
                        P.op("dve", lambda e: e.tensor_tensor(out=keff[:], in0=K[:], in1=tmp[:], op=ALU.mult), [K.h(), tmp.h()], [keff.h()])
                        P.op("dve", lambda e: e.tensor_tensor_scan(out=clw[:], data0=keepm[:], data1=lw[:], initial=0.0, op0=ALU.mult, op1=ALU.add), [lw.h()], [clw.h()])
                        P.op("dve", lambda e: e.tensor_tensor(out=dd[:], in0=clw[:], in1=lw[:], op=ALU.subtract), [clw.h(), lw.h()], [dd.h()])
                        P.op("act", lambda e: e.activation(out=E1[:], in_=clw[:], func=AF.Exp), [clw.h()], [E1.h()])
                        P.op("act", lambda e: e.activation(out=E2[:], in_=dd[:], func=AF.Exp), [dd.h()], [E2.h()])
                        P.op("act", lambda e: e.activation(out=E3[:], in_=clw[:], func=AF.Exp, scale=-1.0), [clw.h()], [E3.h()])
                        P.op("act", lambda e: e.copy(out=WC[:], in_=v3(E1)[:, :, 63]), [E1.h()], [WC.h()])
                        P.op("act", lambda e: e.copy(out=V16[:], in_=V[:]), [V.h()], [V16.h()])
                        P.op("dve", lambda e: e.tensor_tensor(out=AR32[:, :, 64:128], in0=v3(R), in1=v3(E1), op=ALU.mult), [R.h(), E1.h()], [AR32.h()])
                        P.op("dve", lambda e: e.scalar_tensor_tensor(out=rk[:], in0=R[:], scalar=vecs[:, V_RK + hp:V_RK + hp + 1], in1=keff[:], op0=ALU.mult, op1=ALU.mult), [R.h(), keff.h()], [rk.h()])
                        P.op("dve", lambda e: e.tensor_tensor(out=BK32[:, :, 64:128], in0=v3(keff), in1=v3(E3), op=ALU.mult), [keff.h(), E3.h()], [BK32.h()])
                        wcb = WC[:].unsqueeze(2).broadcast_to([128, 8, 64])
                        P.op("dve", lambda e, wcb=wcb: e.tensor_tensor(out=KH[:], in0=BK32[:, :, 64:128], in1=wcb, op=ALU.mult), [BK32.h(), WC.h()], [KH.h()])
                        P.op("dve", lambda e: e.tensor_scalar(out=sd[:], in0=sd[:], scalar1=1e-12, scalar2=None, op0=ALU.max), [sd.h()], [sd.h()])
                        P.op("dve", lambda e: e.reciprocal(out=sd[:], in_=sd[:]), [sd.h()], [sd.h()])
                        P.op("dve", lambda e: e.tensor_tensor(out=kkn[:], in0=kk[:], in1=sd[:], op=ALU.mult), [kk.h(), sd.h()], [kkn.h()])
                        P.op("dve", lambda e: e.tensor_tensor(out=bvec[:], in0=kkn[:], in1=At[:], op=ALU.mult), [kkn.h(), At.h()], [bvec.h()])
                        P.op("dve", lambda e: e.scalar_tensor_tensor(out=AR32[:, :, 0:64], in0=v3(kkn), scalar=-1.0, in1=v3(E2), op0=ALU.mult, op1=ALU.mult), [kkn.h(), E2.h()], [AR32.h()])
                        P.op("act", lambda e: e.copy(out=AR[:], in_=AR32[:]), [AR32.h()], [AR.h()])
                        P.op("dve", lambda e: e.tensor_tensor(out=BK32[:, :, 0:64], in0=v3(bvec), in1=v3(E3), op=ALU.mult), [bvec.h(), E3.h()], [BK32.h()])
                        P.op("act", lambda e: e.copy(out=BK[:], in_=BK32[:]), [BK32.h()], [BK.h()])
                        P.op("dve", lambda e, wcb=wcb: e.tensor_tensor(out=BH[:], in0=BK32[:, :, 0:64], in1=wcb, op=ALU.mult), [BK32.h(), WC.h()], [BH.h()])
                        bx = nb()
                        by = nb()
                        pvx = psb[bx][:].bitcast(BF16)
                        pvy = psb[by][:].bitcast(BF16)

                        def ftr(e, pvx=pvx, pvy=pvy):
                            for c in range(8):
                                e.transpose(pvx[0:64, c * 128:(c + 1) * 128], BH[:, c, :], ident[:])
                                e.transpose(pvx[64:128, c * 128:(c + 1) * 128], KH[:, c, :], ident[:])
                                i = e.transpose(pvy[64:128, c * 128:(c + 1) * 128], V16[:, c * 64:(c + 1) * 64], ident[:])
                            return i
                        P.op("pe", ftr, [BH.h(), KH.h(), V16.h()], [pbh[bx], pbh[by]])
                        P.op("act", lambda e, pvx=pvx: e.copy(out=TOKB[:].rearrange("p c k -> p (c k)"), in_=pvx), [pbh[bx]], [TOKB.h()])
                        P.op("dve", lambda e, pvy=pvy: e.tensor_copy(out=V0[64:128].rearrange("p c k -> p (c k)"), in_=pvy[64:128, :]), [pbh[by]], [V0.h()])
                        P.op("dve", lambda e, pvy=pvy: e.tensor_copy(out=UV[64:128].rearrange("p c k -> p (c k)"), in_=pvy[64:128, :]), [pbh[by]], [UV.h()])
                        ba = nb()
                        pva = psb[ba][:].bitcast(BF16)

                        def fta(e, pva=pva):
                            for c in range(8):
                                i = e.transpose(pva[0:64, c * 128:(c + 1) * 128], AR[:, c, 0:64], ident[:])
                            return i
                        P.op("pe", fta, [AR.h()], [pbh[ba]])
                        P.op("act", lambda e, pva=pva: e.copy(out=ATOK[:].rearrange("p c k -> p (c k)"), in_=pva[0:64, :]), [pbh[ba]], [ATOK.h()])
                        def inv_gen(par, Nn, Pp, Xx):
                            pr = slice(64 * par, 64 * par + 64)
                            for g4 in range(2):
                                b = nb()

                                def fg(e, b=b, g4=g4, pr=pr):
                                    for cc in range(4):
                                        c = g4 * 4 + cc
                                        i = e.matmul(psb[b][:, cc * 128:(cc + 1) * 128], lhsT=BK32[pr, c, :], rhs=AR32[pr, c, :], start=True, stop=True)
                                    return i
                                P.op("pe", fg, [BK32.h(), AR32.h()], [pbh[b]])
                                mgb = MG[:].unsqueeze(1).broadcast_to([128, 4, 128])
                                P.op("dve", lambda e, b=b, g4=g4, par=par, mgb=mgb: e.tensor_tensor(out=Gs[:, par, g4 * 4:(g4 + 1) * 4, :], in0=psb[b][:].rearrange("p (c k) -> p c k", k=128), in1=mgb, op=ALU.mult),
                                     [pbh[b]], [Gs.h(par)])
                            b = nb()

                            def fn_(e, b=b, pr=pr):
                                for c in range(8):
                                    i = e.matmul(psb[b][0:64, c * 64:(c + 1) * 64], lhsT=AR[pr, c, 0:64], rhs=BK[pr, c, 0:64], start=True, stop=True)
                                return i
                            P.op("pe", fn_, [AR.h(), BK.h()], [pbh[b]])
                            mnb = MN[:].unsqueeze(1).broadcast_to([64, 8, 64])
                            P.op("dve", lambda e, b=b, mnb=mnb: e.tensor_tensor(out=Nn[0][:], in0=psb[b][0:64, :].rearrange("p (c k) -> p c k", k=64), in1=mnb, op=ALU.mult), [pbh[b]], [Nn[0].h()])
                            yield
                            bw = nb()

                            def fw00(e, b=bw, par=par):
                                for c in range(8):
                                    i = e.matmul(psb[b][0:64, c * 64:(c + 1) * 64], lhsT=Gs[:, par, c, 0:64], rhs=V0[:, c, par * 64:(par + 1) * 64], start=True, stop=True)
                                return i
                            P.op("pe", fw00, [Gs.h(par), V0.h()], [pbh[bw]])
                            P.op("dve", lambda e, b=bw, par=par: e.tensor_copy(out=W00s[:, par, :, :], in_=psb[b][0:64, :].rearrange("p (c k) -> p c k", k=64)), [pbh[bw]], [W00s.h()])
                            P.op("act", lambda e, par=par: e.copy(out=Pp[0][:], in_=Gs[0:64, par, :, 0:64]), [Gs.h(par)], [Pp[0].h()])
                            idb = ident[0:64, 0:64].unsqueeze(1).broadcast_to([64, 8, 64])
                            P.op("pool", lambda e, idb=idb: e.tensor_tensor(out=Xx[0][:], in0=Pp[0][:], in1=idb, op=ALU.add), [Pp[0].h()], [Xx[0].h()])
                            yield
                            cur = 0
                            for lv in range(1, 6):
                                nx = 1 - cur
                                Pc, Nc, Xc = Pp[cur], Nn[cur], Xx[cur]
                                Pn, Nx, Xn = Pp[nx], Nn[nx], Xx[nx]
                                if lv <= 4:
                                    b = nb()

                                    def fp(e, b=b, Pc=Pc, Nc=Nc):
                                        for c in range(8):
                                            i = e.matmul(psb[b][0:64, c * 64:(c + 1) * 64], lhsT=Nc[:, c, :], rhs=Pc[:, c, :], start=True, stop=True)
                                        return i
                                    P.op("pe", fp, [Pc.h(), Nc.h()], [pbh[b]])
                                    P.op("act", lambda e, b=b, Pn=Pn: e.copy(out=Pn[:].rearrange("p c k -> p (c k)"), in_=psb[b][0:64, :]), [pbh[b]], [Pn.h()])
                                b = nb()

                                def fnn(e, b=b, Pc=Pc, Nc=Nc):
                                    for c in range(8):
                                        i = e.matmul(psb[b][0:64, c * 64:(c + 1) * 64], lhsT=Pc[:, c, :], rhs=Nc[:, c, :], start=True, stop=True)
                                    return i
                                P.op("pe", fnn, [Pc.h(), Nc.h()], [pbh[b]])
                                P.op("dve", lambda e, b=b, Nx=Nx: e.tensor_copy(out=Nx[:].rearrange("p c k -> p (c k)"), in_=psb[b][0:64, :]), [pbh[b]], [Nx.h()])
                                yield
                                b = nb()

                                def fx(e, b=b, Nx=Nx, Xc=Xc):
                                    for c in range(8):
                                        e.matmul(psb[b][0:64, c * 64:(c + 1) * 64], lhsT=Nx[:, c, :], rhs=Xc[:, c, :], start=True, stop=False)
                                        i = e.matmul(psb[b][0:64, c * 64:(c + 1) * 64], lhsT=ident[0:64, 0:64], rhs=Xc[:, c, :], start=False, stop=True)
                                    return i
                                P.op("pe", fx, [Nx.h(), Xc.h()], [pbh[b]])
                                if lv < 5:
                                    P.op("act", lambda e, b=b, Xn=Xn: e.copy(out=Xn[:].rearrange("p c k -> p (c k)"), in_=psb[b][0:64, :]), [pbh[b]], [Xn.h()])
                                else:
                                    P.op("act", lambda e, b=b, par=par: e.copy(out=TT[:, par, :, :], in_=psb[b][0:64, :].rearrange("p (c k) -> p c k", k=64)), [pbh[b]], [TT.h(par)])
                                cur = nx
                                yield
                        gens = [inv_gen(0, NnA, PpA, XxA), inv_gen(1, NnB, PpB, XxB)]
                        while gens:
                            for g_ in list(gens):
                                try:
                                    next(g_)
                                except StopIteration:
                                    gens.remove(g_)
                        bq = [nb(), nb()]
                        for par in range(2):
                            def fq(e, par=par, b=bq[par]):
                                for c in range(8):
                                    i = e.matmul(psb[b][0:64, c * 64:(c + 1) * 64], lhsT=TT[:, par, c, :], rhs=ATOK[:, c, par * 64:(par + 1) * 64], start=True, stop=True)
                                return i
                            P.op("pe", fq, [TT.h(par), ATOK.h()], [pbh[bq[par]]])
                        bu = [nb(), nb()]
                        for par in range(2):
                            def fu0(e, par=par, b=bu[par]):
                                for c in range(8):
                                    i = e.matmul(psb[b][0:64, c * 64:(c + 1) * 64], lhsT=TT[:, par, c, :], rhs=W00s[:, par, c, :], start=True, stop=True)
                                return i
                            P.op("pe", fu0, [TT.h(par), W00s.h()], [pbh[bu[par]]])
                        P.op("act", lambda e, b=bq[0]: e.copy(out=Qs[:, 0, :, :], in_=psb[b][0:64, :].rearrange("p (c k) -> p c k", k=64)), [pbh[bq[0]]], [Qs.h(0)])
                        P.op("dve", lambda e, b=bq[1]: e.tensor_copy(out=Qs[:, 1, :, :], in_=psb[b][0:64, :].rearrange("p (c k) -> p c k", k=64)), [pbh[bq[1]]], [Qs.h(1)])
                        P.op("act", lambda e, b=bu[0]: e.copy(out=UV[0:64, :, 0:64], in_=psb[b][0:64, :].rearrange("p (c k) -> p c k", k=64)), [pbh[bu[0]]], [UV.h()])
                        P.op("dve", lambda e, b=bu[1]: e.tensor_copy(out=UV[0:64, :, 64:128], in_=psb[b][0:64, :].rearrange("p (c k) -> p c k", k=64)), [pbh[bu[1]]], [UV.h()])
                        bm = nb()

                        def fm(e, b=bm):
                            for par in range(2):
                                pr = slice(64 * par, 64 * par + 64)
                                for c in range(8):
                                    i = e.matmul(psb[b][pr, c * 64:(c + 1) * 64], lhsT=Qs[:, par, c, :], rhs=TOKB[0:64, c, par * 64:(par + 1) * 64], start=True, stop=True)
                            return i
                        P.op("pe", fm, [Qs.h(0), Qs.h(1), TOKB.h()], [pbh[bm]])
                        br = nb()

                        def frp(e, b=br):
                            for par in range(2):
                                pr = slice(64 * par, 64 * par + 64)
                                for c in range(8):
                                    i = e.matmul(psb[b][pr, c * 64:(c + 1) * 64], lhsT=Qs[:, par, c, :], rhs=Gs[0:64, par, c, 64:128], start=True, stop=True)
                            return i
                        P.op("pe", frp, [Qs.h(0), Qs.h(1), Gs.h(0), Gs.h(1)], [pbh[br]])
                        P.op("act", lambda e, b=bm: e.copy(out=MT[:].rearrange("p c k -> p (c k)"), in_=psb[b][:]), [pbh[bm]], [MT.h()])
                        P.op("dve", lambda e, b=br: e.tensor_tensor(out=RP[:], in0=psb[b][:].rearrange("p (c k) -> p c k", k=64), in1=AR32[:, :, 64:128], op=ALU.add), [pbh[br], AR32.h()], [RP.h()])
                        yb = 6
                        bdb = BDm[:]
                        for c in range(8):
                            Sc = S16z[sz_i[0] % 2]
                            Sn = S16z[(sz_i[0] + 1) % 2]
                            sz_i[0] += 1
                            b = nb()

                            def frec(e, b=b, c=c, Sc=Sc):
                                for par in range(2):
                                    pr = slice(64 * par, 64 * par + 64)
                                    o = psb[b][pr, 0:64]
                                    e.matmul(o, lhsT=TOKB[:, c, par * 64:(par + 1) * 64], rhs=UV[:, c, par * 64:(par + 1) * 64], start=True, stop=False)
                                    i = e.matmul(o, lhsT=MT[:, c, :], rhs=Sc[:, par, :], start=False, stop=True)
                                return i
                            P.op("pe", frec, [TOKB.h(), UV.h(), MT.h(), Sc.h()], [pbh[b]])

                            def fy(e, c=c, Sc=Sc):
                                for par in range(2):
                                    pr = slice(64 * par, 64 * par + 64)
                                    o = psb[yb][pr, c * 64:(c + 1) * 64]
                                    e.matmul(o, lhsT=UV[:, c, par * 64:(par + 1) * 64], rhs=Gs[:, par, c, 64:128], start=True, stop=False)
                                    i = e.matmul(o, lhsT=Sc[:, par, :], rhs=RP[:, c, :], start=False, stop=True)
                                return i
                            P.op("pe", fy, [UV.h(), Gs.h(0), Gs.h(1), Sc.h(), RP.h()], [pbh[yb]])
                            P.op("dve", lambda e, b=b, c=c: e.scalar_tensor_tensor(out=S32[:], in0=S32[:], scalar=WC[:, c:c + 1], in1=psb[b][:, 0:64], op0=ALU.mult, op1=ALU.add), [S32.h(), WC.h(), pbh[b]], [S32.h()])
                            P.op("dve", lambda e, Sn=Sn, bdb=bdb: e.tensor_tensor(out=Sn[:], in0=S32[:].unsqueeze(1).broadcast_to([128, 2, 64]), in1=bdb, op=ALU.mult), [S32.h()], [Sn.h()])
                        P.op("act", lambda e: e.copy(out=Yt[:], in_=psb[yb][:]), [pbh[yb]], [Yt.h()])
                        P.op("dve", lambda e: e.tensor_tensor(out=Y2[:], in0=Yt[:], in1=Yt[:], op=ALU.mult), [Yt.h()], [Y2.h()])
                        bm = nb()
                        P.op("pe", lambda e, bm=bm: e.matmul(psb[bm][:], lhsT=onesbd[:], rhs=Yt[:], start=True, stop=True), [Yt.h()], [pbh[bm]])
                        be = nb()
                        P.op("pe", lambda e, be=be: e.matmul(psb[be][:], lhsT=onesbd[:], rhs=Y2[:], start=True, stop=True), [Y2.h()], [pbh[be]])
                        P.op("act", lambda e, bm=bm: e.activation(out=mt[:], in_=psb[bm][:], func=AF.Identity, scale=1.0 / 64), [pbh[bm]], [mt.h()])
                        P.op("dve", lambda e: e.tensor_tensor(out=msq[:], in0=mt[:], in1=mt[:], op=ALU.mult), [mt.h()], [msq.h()])
                        P.op("dve", lambda e, be=be: e.scalar_tensor_tensor(out=var[:], in0=psb[be][:], scalar=1.0 / 64, in1=msq[:], op0=ALU.mult, op1=ALU.subtract), [pbh[be], msq.h()], [var.h()])
                        P.op("act", lambda e: e.activation(out=var[:], in_=var[:], func=AF.Sqrt, bias=64e-5), [var.h()], [var.h()])
                        P.op("dve", lambda e: e.reciprocal(out=var[:], in_=var[:]), [var.h()], [var.h()])
                        P.op("dve", lambda e: e.tensor_tensor(out=yc[:], in0=Yt[:], in1=mt[:], op=ALU.subtract), [Yt.h(), mt.h()], [yc.h()])
                        P.op("dve", lambda e: e.tensor_tensor(out=yc[:], in0=yc[:], in1=var[:], op=ALU.mult), [yc.h(), var.h()], [yc.h()])
                        P.op("dve", lambda e: e.tensor_scalar(out=yc[:], in0=yc[:], scalar1=vecs[:, V_LNW + hp:V_LNW + hp + 1], scalar2=vecs[:, V_LNB + hp:V_LNB + hp + 1], op0=ALU.mult, op1=ALU.add), [yc.h()], [yc.h()])
                        bb = nb()
                        P.op("pe", lambda e, bb=bb: e.matmul(psb[bb][:], lhsT=onesbd[:], rhs=rk[:], start=True, stop=True), [rk.h()], [pbh[bb]])
                        P.op("dve", lambda e, bb=bb: e.tensor_tensor(out=tmp[:], in0=psb[bb][:], in1=V[:], op=ALU.mult), [pbh[bb], V.h()], [tmp.h()])
                        P.op("dve", lambda e: e.tensor_tensor(out=yc[:], in0=yc[:], in1=tmp[:], op=ALU.add), [yc.h(), tmp.h()], [yc.h()])
                        P.op("dve", lambda e, ts=ts: e.tensor_tensor(out=OO[:, 4 + hp, ts], in0=yc[:], in1=Gt[:], op=ALU.mult), [yc.h(), Gt.h()], [OO.h(4 + hp, tb)])
                    for tb in range(NTB):
                        rw_block(hp, tb)
                if sq == dbg_sq:
                    dbg_dump("orw", OO[:, 4:8, :], OO.hs(range(4, 8), range(NTB)), [128, 4, S], BF16)
                P.barrier()

            if stage == 'C':
                break
            with ExitStack() as ph:
                rings(ph, 8, 0)
                sg_t = [sb([128, 512], F32, "sg_t", ph) for _ in range(2)]
                m_t = [sb([128, 512], F32, "m_t", ph) for _ in range(2)]
                for cb in range(8):
                    wgs = loadA(sA_in[26 + cb], "gates")
                    wgr = loadA(sA_in[34 + cb], "gates")
                    wus = loadA(sA_us[cb], "us", 4)
                    wur = loadA(sA_ur[cb], "ur", 4)
                    for tb in range(NTB):
                        T0 = tb * 512
                        hTh = hT.hs(range(8), [tb])
                        bgs = projA(wgs, 8, hT, T0, hTh)
                        P.op("act", lambda e, bgs=bgs: e.activation(out=sg_t[0][:], in_=psb[bgs][:], func=AF.Sigmoid), [pbh[bgs]], [sg_t[0].h()])
                        bgr = projA(wgr, 8, hT, T0, hTh)
                        P.op("act", lambda e, bgr=bgr: e.activation(out=sg_t[1][:], in_=psb[bgr][:], func=AF.Sigmoid), [pbh[bgr]], [sg_t[1].h()])
                        bus = nb()

                        def f1(e, bus=bus, wus=wus, T0=T0):
                            for kc in range(4):
                                i = e.matmul(psb[bus][:], lhsT=wus[:, kc, :], rhs=OO[:, kc, T0:T0 + 512], start=(kc == 0), stop=(kc == 3))
                            return i
                        P.op("pe", f1, [wus.h()] + OO.hs(range(4), [tb]), [pbh[bus]])
                        P.op("dve", lambda e, bus=bus: e.tensor_tensor(out=m_t[0][:], in0=psb[bus][:], in1=sg_t[0][:], op=ALU.mult), [pbh[bus], sg_t[0].h()], [m_t[0].h()])
                        bur = nb()

                        def f2(e, bur=bur, wur=wur, T0=T0):
                            for kc in range(4):
                                i = e.matmul(psb[bur][:], lhsT=wur[:, kc, :], rhs=OO[:, 4 + kc, T0:T0 + 512], start=(kc == 0), stop=(kc == 3))
                            return i
                        P.op("pe", f2, [wur.h()] + OO.hs(range(4, 8), [tb]), [pbh[bur]])
                        P.op("dve", lambda e, bur=bur: e.tensor_tensor(out=m_t[1][:], in0=psb[bur][:], in1=sg_t[1][:], op=ALU.mult), [pbh[bur], sg_t[1].h()], [m_t[1].h()])
                        P.op("pool", lambda e, cb=cb, T0=T0: e.tensor_tensor(out=QK[:, cb, T0:T0 + 512], in0=m_t[0][:], in1=m_t[1][:], op=ALU.add), [m_t[0].h(), m_t[1].h()], [QK.h(cb, tb)])
                P.barrier()

            if stage == 'D1':
                break
            for hf in range(2):
                H0 = hf * HALF
                with ExitStack() as ph:
                    rings(ph, 4, 3)
                    x2 = sb([128, NTH, D], F32, "x2", ph)
                    L = dict(i=0, ss=[sb([128, 4], F32, "ss", ph) for _ in range(4)], junk=sb([128, D], BF16, "junk", ph),
                             hb=[sb([128, D], BF16, "hb", ph) for _ in range(2)])
                    for tt in range(NTH):
                        P.dma("sp", lambda e, s, tt=tt: e.dma_start(out=x2[:, tt, :], in_=x_d[sq, H0 + tt * 128:H0 + (tt + 1) * 128, :]).then_inc(s, 16), [], [x2.h(tt)], "x2_%d" % tt)
                    ws_ = [loadB(sB_out[cbo], "out") for cbo in range(2)]
                    for tt in range(NTH + 1):
                        if tt < NTH:
                            for cbo in range(2):
                                w = ws_[cbo]
                                b = nb()
                                tg = H0 + tt * 128

                                def f(e, b=b, w=w, tg=tg):
                                    for kc in range(8):
                                        i = e.matmul(psb[b][:], lhsT=QK[:, kc, tg:tg + 128], rhs=w[:, kc, :], start=(kc == 0), stop=(kc == 7))
                                    return i
                                P.op("pe", f, [w.h()] + QK.hs(range(8), [tg // 512]), [pbh[b]])
                                o = x2[:, tt, cbo * 512:(cbo + 1) * 512]
                                P.op("dve", lambda e, b=b, o=o: e.tensor_tensor(out=o, in0=psb[b][:], in1=o, op=ALU.add), [pbh[b], x2.h(tt)], [x2.h(tt)])
                        if tt >= 1:
                            norm_transpose(x2[:, tt - 1, :], x2.h(tt - 1), V_GM, hT, (H0 // 128) + tt - 1, L)
                    relu_t = [sb([128, 512], F32, "relu_t", ph) for _ in range(2)]
                    rl_i = [0]
                    for g in range(4):
                        aT = OO
                        A0 = (g % 2) * HALF
                        for c8 in range(8):
                            w = loadA(sA_ff1[g * 8 + c8], "ff1_%d" % g)
                            for tb in range(NTBH):
                                b = projA(w, 8, hT, H0 + tb * 512, hT.hs(range(8), [(H0 + tb * 512) // 512]))
                                o = OO[:, c8, A0 + tb * 512:A0 + (tb + 1) * 512]
                                wr = [OO.h(c8, (A0 + tb * 512) // 512)]
                                rl = relu_t[rl_i[0] % 2]
                                rl_i[0] += 1
                                P.op("act", lambda e, b=b, rl=rl: e.activation(out=rl[:], in_=psb[b][:], func=AF.Relu), [pbh[b]], [rl.h()])
                                P.op("pool", lambda e, o=o, rl=rl: e.tensor_tensor(out=o, in0=rl[:], in1=rl[:], op=ALU.mult), [rl.h()], wr)
                        ws2_ = [loadB(sB_ff2[g, cbo], "ff2_%d" % g) for cbo in range(2)]
                        for tt in range(NTH + 1):
                            if tt < NTH:
                                for cbo in range(2):
                                    w = ws2_[cbo]
                                    b = nb()
                                    ta = A0 + tt * 128

                                    def f(e, b=b, w=w, ta=ta):
                                        for kc in range(8):
                                            i = e.matmul(psb[b][:], lhsT=OO[:, kc, ta:ta + 128], rhs=w[:, kc, :], start=(kc == 0), stop=(kc == 7))
                                        return i
                                    P.op("pe", f, [w.h()] + OO.hs(range(8), [ta // 512]), [pbh[b]])
                                    o = x2[:, tt, cbo * 512:(cbo + 1) * 512]
                                    P.op("dve", lambda e, b=b, o=o: e.tensor_tensor(out=o, in0=psb[b][:], in1=o, op=ALU.add), [pbh[b], x2.h(tt)], [x2.h(tt)])
                            if g == 3 and tt >= 1:
                                norm_transpose(x2[:, tt - 1, :], x2.h(tt - 1), V_GP, hT, (H0 // 128) + tt - 1, L)
                    pt = [sb([128, 256], F32, "pt", ph) for _ in range(2)]
                    pb16 = [sb([128, 256], BF16, "pb16", ph) for _ in range(2)]
                    pT = sb([128, 2, HALF], BF16, "pT", ph)
                    for tt in range(NTH):
                        ptt = pt[tt % 2]
                        pbt = pb16[tt % 2]
                        P.dma("sp", lambda e, s, ptt=ptt, tt=tt: e.dma_start(out=ptt[:], in_=p_d[sq, H0 + tt * 128:H0 + (tt + 1) * 128, :]).then_inc(s, 16), [], [ptt.h()], "pt%d" % (tt % 2))
                        P.op("pool", lambda e, ptt=ptt, pbt=pbt: e.tensor_copy(out=pbt[:], in_=ptt[:]), [ptt.h()], [pbt.h()])
                        b = nb()
                        pv = psb[b][:].bitcast(BF16)

                        def ftp(e, pv=pv, pbt=pbt):
                            e.transpose(pv[:, 0:128], pbt[:, 0:128], ident[:])
                            return e.transpose(pv[:, 128:256], pbt[:, 128:256], ident[:])
                        P.op("pe", ftp, [pbt.h()], [pbh[b]])
                        P.op("act", lambda e, pv=pv, tt=tt: e.copy(out=pT[:, :, tt * 128:(tt + 1) * 128], in_=pv[:, 0:256].rearrange("p (k t) -> p k t", t=128)), [pbh[b]], [pT.h(tt)])
                    sgp = [sb([128, 512], F32, "sgp", ph) for _ in range(2)]
                    for cbo in range(2):
                        wg = loadB(sB_pg[cbo], "pg")
                        wp = loadB(sB_pp[cbo], "pp", 2)
                        for tt in range(NTH):
                            tg = H0 + tt * 128
                            bg = nb()

                            def f(e, bg=bg, wg=wg, tg=tg):
                                for kc in range(8):
                                    i = e.matmul(psb[bg][:], lhsT=hT[:, kc, tg:tg + 128], rhs=wg[:, kc, :], start=(kc == 0), stop=(kc == 7))
                                return i
                            P.op("pe", f, [wg.h()] + hT.hs(range(8), [tg // 512]), [pbh[bg]])
                            s_ = sgp[tt % 2]
                            P.op("act", lambda e, bg=bg, s_=s_: e.activation(out=s_[:], in_=psb[bg][:], func=AF.Sigmoid), [pbh[bg]], [s_.h()])
                            bp = nb()

                            def f2(e, bp=bp, wp=wp, tt=tt):
                                for kc in range(2):
                                    i = e.matmul(psb[bp][:], lhsT=pT[:, kc, tt * 128:(tt + 1) * 128], rhs=wp[:, kc, :], start=(kc == 0), stop=(kc == 1))
                                return i
                            P.op("pe", f2, [wp.h(), pT.h(tt)], [pbh[bp]])
                            P.op("dve", lambda e, bp=bp, s_=s_: e.tensor_tensor(out=s_[:], in0=psb[bp][:], in1=s_[:], op=ALU.mult), [pbh[bp], s_.h()], [s_.h()])
                            o = x2[:, tt, cbo * 512:(cbo + 1) * 512]
                            P.op("pool", lambda e, o=o, s_=s_: e.tensor_tensor(out=o, in0=o, in1=s_[:], op=ALU.add), [s_.h(), x2.h(tt)], [x2.h(tt)])
                    ot = [sb([128, D], F32, "ot", ph) for _ in range(2)]
                    for tt in range(NTH):
                        ss = L["ss"][L["i"] % 4]
                        L["i"] += 1
                        junk = L["junk"]
                        o_ = ot[tt % 2]
                        xa = x2[:, tt, :]
                        P.op("act", lambda e, xa=xa, ss=ss: e.activation(out=junk[:], in_=xa, func=AF.Square, accum_out=ss[:, 0:1]), [x2.h(tt)], [junk.h(), ss.h()])
                        P.op("act", lambda e, ss=ss: e.activation(out=ss[:, 1:2], in_=ss[:, 0:1], func=AF.Sqrt, scale=1.0 / D, bias=1e-6), [ss.h()], [ss.h()])
                        P.op("dve", lambda e, ss=ss: e.reciprocal(out=ss[:, 2:3], in_=ss[:, 1:2]), [ss.h()], [ss.h()])
                        P.op("dve", lambda e, xa=xa, ss=ss, o_=o_: e.scalar_tensor_tensor(out=o_[:], in0=xa, scalar=ss[:, 2:3], in1=gF[:], op0=ALU.mult, op1=ALU.mult), [x2.h(tt), ss.h()], [o_.h()])
                        P.dma("sp", lambda e, s, o_=o_, tt=tt: e.dma_start(out=out_d[sq, H0 + tt * 128:H0 + (tt + 1) * 128, :], in_=o_[:]).then_inc(s, 16), [o_.h()], [], "st%d" % (tt % 2))
                    P.barrier()
        P.finish()
        ninst = P.ninst
    return nc, dbg_d, ninst


def prep_weights(inp):
    f = lambda a: np.ascontiguousarray(np.asarray(a, dtype=np.float32))
    w_in = f(inp["w_in"])[0]
    d = {}
    d["wA_in"] = f(w_in.reshape(8, 128, 42, 128).transpose(2, 1, 0, 3))
    d["wB_v"] = f(w_in[:, 1024:1536].reshape(8, 128, 512).transpose(1, 0, 2))
    d["wA_us"] = f(f(inp["w_up_sb"])[0].reshape(4, 128, 8, 128).transpose(2, 1, 0, 3))
    d["wA_ur"] = f(f(inp["w_up_rw"])[0].reshape(4, 128, 8, 128).transpose(2, 1, 0, 3))
    d["wB_out"] = f(f(inp["w_out"])[0].reshape(8, 128, 2, 512).transpose(2, 1, 0, 3))
    d["wA_ff1"] = f(f(inp["w_ff1"])[0].reshape(8, 128, 32, 128).transpose(2, 1, 0, 3))
    d["wB_ff2"] = f(f(inp["w_ff2"])[0].reshape(4, 8, 128, 2, 512).transpose(0, 3, 2, 1, 4))
    d["wB_pg"] = f(f(inp["w_ple_gate"])[0].reshape(8, 128, 2, 512).transpose(2, 1, 0, 3))
    d["wB_pp"] = f(f(inp["w_ple_proj"])[0].reshape(2, 128, 2, 512).transpose(2, 1, 0, 3))
    d["w2a2"] = f(np.concatenate([f(inp["decay_w2"])[0], f(inp["iclr_a2"])[0]], axis=0))
    d["g2"] = f(inp["gate_g2"])[0]
    vecs = np.zeros((128, NV), np.float32)
    col = lambda v, n: f(v).reshape(n, 128).T
    vecs[:, V_GA:V_GA + 8] = col(inp["attn_norm_g"], 8)
    vecs[:, V_GM:V_GM + 8] = col(inp["mlp_norm_g"], 8)
    vecs[:, V_GP:V_GP + 8] = col(inp["ple_norm_g"], 8)
    vecs[:, V_MU:V_MU + 14] = col(inp["shift_mu"], 14)
    vecs[:, V_W0:V_W0 + 4] = col(inp["decay_w0"], 4)
    vecs[:, V_A0:V_A0 + 4] = col(inp["iclr_a0"], 4)
    vecs[:, V_KK:V_KK + 4] = col(inp["k_k"], 4)
    vecs[:, V_KA:V_KA + 4] = col(inp["k_a"], 4)
    vecs[:, V_RK:V_RK + 4] = col(inp["r_k"], 4)
    vecs[:, V_LNW:V_LNW + 4] = col(inp["ln_x_w"], 4)
    vecs[:, V_LNB:V_LNB + 4] = col(inp["ln_x_b"], 4)
    d["vecs"] = vecs
    d["gF"] = f(np.broadcast_to(f(inp["final_norm_g"]).reshape(1, D), (128, D)))
    return d


def kernel(**inputs):
    x = np.asarray(inputs["x"], dtype=np.float32)
    p = np.asarray(inputs["p"], dtype=np.float32)[0]
    B, S, _ = x.shape
    NB = B // NCORES
    nc, _, _ = build(S, NB)
    wd = prep_weights(inputs)
    in_maps = []
    for c in range(NCORES):
        m = dict(wd)
        m["x"] = np.ascontiguousarray(x[c * NB:(c + 1) * NB])
        m["p"] = np.ascontiguousarray(p[c * NB:(c + 1) * NB])
        m["x0T"] = np.ascontiguousarray(x[c * NB:(c + 1) * NB, 0, :].reshape(NB, 8, 128).transpose(0, 2, 1))
        in_maps.append(m)
    res = run_bass_kernel_spmd(nc, in_maps, core_ids=list(range(NCORES)))
    return np.concatenate([np.asarray(r["out"], dtype=np.float32) for r in res.results], axis=0)
```

```python
import numpy as np
from contextlib import ExitStack
import concourse.bass as bass
import concourse.mybir as mybir
from concourse.bass_utils import run_bass_kernel_spmd

F32 = mybir.dt.float32
BF16 = mybir.dt.bfloat16
AF = mybir.ActivationFunctionType
ALU = mybir.AluOpType

D = 1024
NCORES = 8
SEM_CAP = 16000
NV = 70
(V_GA, V_GM, V_GP, V_MU, V_W0, V_A0, V_KK, V_KA, V_RK, V_LNW, V_LNB) = (0, 8, 16, 24, 38, 42, 46, 50, 54, 58, 62)
V_OMU = 66


class Hd:
    __slots__ = ("name", "lw", "rdc", "rdd")

    def __init__(self, name):
        self.name = name
        self.lw = None
        self.rdc = {}
        self.rdd = []


class Op:
    __slots__ = ("eng", "fn", "deps", "marked", "sem", "val", "is_dma", "n", "epoch", "persist")


class Prog:
    CE = ("pe", "act", "dve", "pool")

    def __init__(self, nc, es):
        self.nc = nc
        self.es = es
        self.engs = dict(pe=nc.tensor, act=nc.scalar, dve=nc.vector, pool=nc.gpsimd, sp=nc.sync)
        self.ops = []
        self.csems = {e: [] for e in self.CE}
        self.ccount = {e: 0 for e in self.CE}
        self.known = {e: {} for e in self.engs}
        self.dsem = {}
        self.last = {e: None for e in self.CE}
        self.dma_live = []
        self.pending = {e: [] for e in self.engs}
        self.epoch = 0
        self.nsem = 0
        self.ninst = 0

    def _newsem(self, name):
        self.nsem += 1
        return self.es.enter_context(self.nc.semaphore(name))

    def _mk(self, eng, fn, reads, writes, is_dma, n=1, key=None):
        o = Op()
        o.eng = eng
        o.fn = fn
        o.marked = False
        o.is_dma = is_dma
        o.n = n
        o.epoch = self.epoch
        o.persist = False
        o.sem = None
        o.val = 0
        deps = {}

        def add(d):
            if d is None or d is o or (d.epoch != self.epoch and not d.persist):
                return
            if (not d.is_dma) and (not is_dma) and d.eng == "pe" and eng == "pe":
                return
            deps[id(d)] = d

        for h in reads:
            add(h.lw)
        for h in writes:
            add(h.lw)
            for r in h.rdc.values():
                add(r)
            for r in h.rdd:
                add(r)
        for d in self.pending[eng]:
            deps[id(d)] = d
        self.pending[eng] = []
        o.deps = list(deps.values())
        for d in o.deps:
            d.marked = True
        for h in writes:
            h.lw = o
            h.rdc = {}
            h.rdd = []
        for h in reads:
            if h.lw is o:
                continue
            if is_dma:
                h.rdd.append(o)
            else:
                h.rdc[eng] = o
        if is_dma:
            if key not in self.dsem:
                self.dsem[key] = [self._newsem("d_" + key), 0]
            ds = self.dsem[key]
            ds[1] += 16 * n
            o.sem = ds[0]
            o.val = ds[1]
            self.dma_live.append(o)
        else:
            self.last[eng] = o
        self.ops.append(o)
        return o

    def op(self, eng, fn, reads=(), writes=()):
        return self._mk(eng, fn, reads, writes, False)

    def dma(self, eng, fn, reads, writes, key, n=1, persist=False):
        o = self._mk(eng, fn, reads, writes, True, n=n, key=key)
        if persist:
            o.persist = True
            self.dma_live.remove(o)
        return o

    def flush(self):
        for o in self.ops:
            if (not o.is_dma) and o.marked:
                self.ccount[o.eng] += 1
                r = self.ccount[o.eng] - 1
                ei = r // SEM_CAP
                while len(self.csems[o.eng]) <= ei:
                    self.csems[o.eng].append(self._newsem("c_%s%d" % (o.eng, len(self.csems[o.eng]))))
                o.sem = self.csems[o.eng][ei]
                o.val = r % SEM_CAP + 1
        for o in self.ops:
            e = self.engs[o.eng]
            need = {}
            for d in o.deps:
                k = id(d.sem)
                if k not in need or need[k][1] < d.val:
                    need[k] = (d.sem, d.val)
            kn = self.known[o.eng]
            for k, (s, v) in need.items():
                if kn.get(k, 0) < v:
                    e.wait_ge(s, v)
                    kn[k] = v
                    self.ninst += 1
            if o.is_dma:
                o.fn(e, o.sem)
            else:
                inst = o.fn(e)
                if o.marked:
                    inst.then_inc(o.sem, 1)
            self.ninst += 1
        self.ops = []

    def barrier(self):
        deps = [o for o in self.last.values() if o is not None and o.epoch == self.epoch]
        deps += self.dma_live
        for d in deps:
            d.marked = True
        self.flush()
        self.epoch += 1
        self.dma_live = []
        for e in self.engs:
            self.pending[e] = list(deps)

    def finish(self):
        self.barrier()
        sp = self.engs["sp"]
        need = {}
        for d in self.pending["sp"]:
            k = id(d.sem)
            if k not in need or need[k][1] < d.val:
                need[k] = (d.sem, d.val)
        for k, (s, v) in need.items():
            sp.wait_ge(s, v)


class Buf:
    def __init__(self, t, name):
        self.t = t
        self.name = name
        self.hd = {}

    def h(self, *key):
        if key not in self.hd:
            self.hd[key] = Hd(self.name + str(key))
        return self.hd[key]

    def hs(self, *ranges):
        import itertools
        return [self.h(*k) for k in itertools.product(*ranges)]

    def __getitem__(self, k):
        return self.t[k]


class _Stop(Exception):
    pass


def build(S, NB, dbg_names=(), stage=None, dbg_sq=0):
    nc = bass.Bass("TRN2", target_bir_lowering=False)
    NT = S // 128
    NTB = S // 512
    HALF = S // 2
    NTH = HALF // 128
    NTBH = HALF // 512
    assert HALF % 512 == 0

    def din(name, shape):
        return nc.dram_tensor(name, list(shape), F32, kind="ExternalInput").ap()

    x_d = din("x", [NB, S, D])
    p_d = din("p", [NB, S, 256])
    x0T_d = din("x0T", [NB, 128, 8])
    wA_in = din("wA_in", [42, 128, 8, 128])
    wB_v = din("wB_v", [128, 8, 512])
    wA_us = din("wA_us", [8, 128, 4, 128])
    wA_ur = din("wA_ur", [8, 128, 4, 128])
    wB_out = din("wB_out", [2, 128, 8, 512])
    wA_ff1 = din("wA_ff1", [32, 128, 8, 128])
    wB_ff2 = din("wB_ff2", [4, 2, 128, 8, 512])
    wB_pg = din("wB_pg", [2, 128, 8, 512])
    wB_pp = din("wB_pp", [2, 128, 2, 512])
    w2a2_d = din("w2a2", [128, 512])
    g2_d = din("g2", [128, 512])
    vecs_d = din("vecs", [128, NV])
    gF_d = din("gF", [128, D])
    out_d = nc.dram_tensor("out", [NB, S, D], F32, kind="ExternalOutput").ap()
    dbg_d = {}

    with ExitStack() as es:
        P = Prog(nc, es)
        cnt = [0]

        def sb(shape, dt, name=None, stack=None):
            cnt[0] += 1
            nm = (name or "t") + "_%d" % cnt[0]
            t = (stack or es).enter_context(nc.sbuf_tensor(nm, list(shape), dt))
            return Buf(t, nm)

        psb = [es.enter_context(nc.psum_tensor("ps%d" % i, [128, 512], F32)) for i in range(8)]
        pbh = [Hd("psb%d" % i) for i in range(8)]
        rot = [0]

        def nb():
            b = rot[0]
            rot[0] = (rot[0] + 1) % 6
            return b

        ident = sb([128, 128], BF16, "ident")
        ones16 = sb([128, 128], BF16, "ones16")
        stri = sb([128, 128], BF16, "stri")
        masks4 = sb([128, 4, 512], BF16, "masks4")
        ones512 = sb([128, 512], BF16, "ones512")
        onesbd = sb([128, 128], F32, "onesbd")
        keepm = sb([128, 512], F32, "keepm")
        MG = sb([128, 128], BF16, "MG")
        MN = sb([64, 64], BF16, "MN")
        vecs = sb([128, NV], F32, "vecs")
        omu = sb([128, 14], F32, "omu")
        oka = sb([128, 4], F32, "oka")
        gF = sb([128, D], F32, "gF")
        w2a2 = sb([128, 512], BF16, "w2a2")
        g2 = sb([128, 512], BF16, "g2")
        hc = Hd("consts")

        h1, h5, hid, hst, hm4, hbd, hkm, hmg, hmn = [Hd("c%d" % i) for i in range(9)]
        P.op("pool", lambda e: e.memset(ones16[:], 1.0), [], [h1])
        P.op("pool", lambda e: e.memset(ones512[:], 1.0), [], [h5])
        P.op("pool", lambda e: e.affine_select(out=ident[:], in_=ones16[:], pattern=[[-1, 128]], compare_op=ALU.is_equal, fill=0.0, base=0, channel_multiplier=1), [h1], [hid])
        P.op("pool", lambda e: e.affine_select(out=stri[:], in_=ones16[:], pattern=[[-1, 128]], compare_op=ALU.is_gt, fill=0.0, base=0, channel_multiplier=1), [h1], [hst])
        for j in range(4):
            P.op("pool", lambda e, j=j: e.affine_select(out=masks4[:, j, :], in_=ones512[:], pattern=[[1, 512]], compare_op=ALU.is_gt, fill=0.0, base=-128 * j, channel_multiplier=-1), [h5], [hm4])
        P.op("pool", lambda e: e.memset(onesbd[:], 0.0), [], [hbd])
        P.op("pool", lambda e: e.memset(onesbd[0:64, 0:64], 1.0), [], [hbd])
        P.op("pool", lambda e: e.memset(onesbd[64:128, 64:128], 1.0), [], [hbd])
        P.op("pool", lambda e: e.memset(keepm[:], 1.0), [], [hkm])
        P.op("pool", lambda e: e.affine_select(out=keepm[:].rearrange("p (c t) -> p c t", t=64), in_=keepm[:].rearrange("p (c t) -> p c t", t=64),
                                               pattern=[[0, 8], [1, 64]], compare_op=ALU.not_equal, fill=0.0, base=0, channel_multiplier=0), [hkm], [hkm])
        for r0 in (0, 64):
            P.op("pool", lambda e, r0=r0: e.affine_select(out=MG[r0:r0 + 64, 0:64], in_=ones16[r0:r0 + 64, 0:64], pattern=[[1, 64]], compare_op=ALU.is_gt, fill=0.0, base=0, channel_multiplier=-1), [h1], [hmg])
            P.op("pool", lambda e, r0=r0: e.affine_select(out=MG[r0:r0 + 64, 64:128], in_=ones16[r0:r0 + 64, 0:64], pattern=[[1, 64]], compare_op=ALU.is_ge, fill=0.0, base=0, channel_multiplier=-1), [h1], [hmg])
        P.op("pool", lambda e: e.affine_select(out=MN[:], in_=ones16[0:64, 0:64], pattern=[[-1, 64]], compare_op=ALU.is_gt, fill=0.0, base=0, channel_multiplier=1), [h1], [hmn])
        hv = Hd("vecs")
        P.dma("sp", lambda e, s: e.dma_start(out=vecs[:], in_=vecs_d).then_inc(s, 16), [], [hv], "vecs")
        P.dma("sp", lambda e, s: e.dma_start(out=gF[:], in_=gF_d).then_inc(s, 16), [], [hv], "gF")
        P.dma("pool", lambda e, s: e.dma_start(out=w2a2[:], in_=w2a2_d).then_inc(s, 16), [], [hv], "w2a2")
        P.dma("pool", lambda e, s: e.dma_start(out=g2[:], in_=g2_d).then_inc(s, 16), [], [hv], "g2")
        P.op("dve", lambda e: e.tensor_scalar(out=omu[:], in0=vecs[:, V_MU:V_MU + 14], scalar1=-1.0, scalar2=1.0, op0=ALU.mult, op1=ALU.add), [hv], [hc])
        P.op("dve", lambda e: e.tensor_scalar(out=oka[:], in0=vecs[:, V_KA:V_KA + 4], scalar1=-1.0, scalar2=1.0, op0=ALU.mult, op1=ALU.add), [hv], [hc])
        P.barrier()

        def scr(name, shape):
            return nc.dram_tensor(name, list(shape), BF16, kind="Internal").ap()
        sA_in = scr("sA_in", [42, 128, 8, 128])
        sB_v = scr("sB_v", [128, 8, 512])
        sA_us = scr("sA_us", [8, 128, 4, 128])
        sA_ur = scr("sA_ur", [8, 128, 4, 128])
        sB_out = scr("sB_out", [2, 128, 8, 512])
        sA_ff1 = scr("sA_ff1", [32, 128, 8, 128])
        sB_ff2 = scr("sB_ff2", [4, 2, 128, 8, 512])
        sB_pg = scr("sB_pg", [2, 128, 8, 512])
        sB_pp = scr("sB_pp", [2, 128, 2, 512])
        grp = {}

        def precast(gname, pairs):
            h_ = Hd("grp_" + gname)
            grp[gname] = h_

            def fn(e, s_, pairs=pairs):
                for d_, sr_ in pairs:
                    e.dma_start(out=d_, in_=sr_).then_inc(s_, 16)
            P.dma("pool", fn, [], [h_], "pc_" + gname, n=len(pairs), persist=True)
        precast("qk", [(sA_in[j], wA_in[j]) for j in range(8)])
        precast("v", [(sB_v, wB_v)])
        precast("lora", [(sA_in[j], wA_in[j]) for j in (24, 25)])
        precast("rkv", [(sA_in[j], wA_in[j]) for j in range(12, 24)])
        precast("gates", [(sA_in[j], wA_in[j]) for j in range(26, 42)])
        precast("us", [(sA_us[j], wA_us[j]) for j in range(8)])
        precast("ur", [(sA_ur[j], wA_ur[j]) for j in range(8)])
        precast("out", [(sB_out[j], wB_out[j]) for j in range(2)])
        for g_ in range(4):
            precast("ff1_%d" % g_, [(sA_ff1[j], wA_ff1[j]) for j in range(g_ * 8, g_ * 8 + 8)])
            precast("ff2_%d" % g_, [(sB_ff2[g_, j], wB_ff2[g_, j]) for j in range(2)])
        precast("pg", [(sB_pg[j], wB_pg[j]) for j in range(2)])
        precast("pp", [(sB_pp[j], wB_pp[j]) for j in range(2)])

        hT = sb([128, 8, S], BF16, "hT")
        QK = sb([128, 8, S], BF16, "QK")
        OO = sb([128, 8, S], BF16, "OO")
        WA = []
        WB = []
        wa_i = [0]
        wb_i = [0]

        def rings(ph_, na, nb_):
            WA[:] = [sb([128, 8, 128], BF16, "WA%d" % i, ph_) for i in range(na)]
            WB[:] = [sb([128, 8, 512], BF16, "WB%d" % i, ph_) for i in range(nb_)]
            wa_i[0] = 0
            wb_i[0] = 0

        def loadA(src, g, nk=8):
            i = wa_i[0]
            wa_i[0] = (i + 1) % len(WA)
            w = WA[i]
            P.dma("sp", lambda e, s, w=w, src=src, nk=nk: e.dma_start(out=w[:, 0:nk, :], in_=src).then_inc(s, 16), [grp[g]], [w.h()], "WA%d" % i)
            return w

        def loadB(src, g, nk=8):
            i = wb_i[0]
            wb_i[0] = (i + 1) % len(WB)
            w = WB[i]
            P.dma("sp", lambda e, s, w=w, src=src, nk=nk: e.dma_start(out=w[:, 0:nk, :], in_=src).then_inc(s, 16), [grp[g]], [w.h()], "WB%d" % i)
            return w

        ev_i = [0]

        def evac_copy(out, in_, reads, writes, eng=None):
            if eng is None:
                eng = ("act", "dve")[ev_i[0] % 2]
                ev_i[0] += 1
            if eng == "act":
                P.op("act", lambda e, out=out, in_=in_: e.copy(out=out, in_=in_), reads, writes)
            else:
                P.op("dve", lambda e, out=out, in_=in_: e.tensor_copy(out=out, in_=in_), reads, writes)

        def tbs(t0, n):
            return range(t0 // 512, (t0 + n - 1) // 512 + 1)

        def norm_transpose(src_ap, src_h, gcol, dstT, tt, L):
            ss = L["ss"][L["i"] % 4]
            L["i"] += 1
            junk = L["junk"]
            hb = L["hb"][L["i"] % 2]
            P.op("act", lambda e: e.activation(out=junk[:], in_=src_ap, func=AF.Square, accum_out=ss[:, 0:1]), [src_h], [junk.h(), ss.h()])
            P.op("act", lambda e: e.activation(out=ss[:, 1:2], in_=ss[:, 0:1], func=AF.Sqrt, scale=1.0 / D, bias=1e-6), [ss.h()], [ss.h()])
            P.op("dve", lambda e: e.reciprocal(out=ss[:, 2:3], in_=ss[:, 1:2]), [ss.h()], [ss.h()])
            P.op("dve", lambda e: e.tensor_scalar(out=hb[:], in0=src_ap, scalar1=ss[:, 2:3], scalar2=None, op0=ALU.mult), [src_h, ss.h()], [hb.h()])
            import os
            if os.environ.get('DBG_NOTR'):
                return
            b = nb()
            pv = psb[b][:].bitcast(BF16)

            def tr(e):
                for kc in range(8):
                    i = e.transpose(pv[:, kc * 128:(kc + 1) * 128], hb[:, kc * 128:(kc + 1) * 128], ident[:])
                return i
            P.op("pe", tr, [hb.h()], [pbh[b]])
            if os.environ.get('DBG_NOEV'):
                return
            for kc in range(8):
                o = dstT[:, kc, tt * 128:(tt + 1) * 128]
                i_ = pv[:, kc * 128:(kc + 1) * 128]
                sc = vecs[:, gcol + kc:gcol + kc + 1]
                wr = [dstT.h(kc, (tt * 128) // 512)]
                if tt % 2 == 0:
                    P.op("act", lambda e, o=o, i_=i_, sc=sc: e.activation(out=o, in_=i_, func=AF.Identity, scale=sc), [pbh[b]], wr)
                else:
                    P.op("dve", lambda e, o=o, i_=i_, sc=sc: e.tensor_scalar(out=o, in0=i_, scalar1=sc, scalar2=None, op0=ALU.mult), [pbh[b]], wr)

        def projA(w, nk, rhsT, t0, kcs_handles):
            b = nb()

            def f(e):
                for kc in range(nk):
                    i = e.matmul(psb[b][:], lhsT=w[:, kc, :], rhs=rhsT[:, kc, t0:t0 + 512], start=(kc == 0), stop=(kc == nk - 1))
                return i
            P.op("pe", f, [w.h()] + kcs_handles, [pbh[b]])
            return b

        def dbg_dump(name, ap, hlist, shape, dt=F32):
            if name not in dbg_names:
                return
            d = nc.dram_tensor("dbg_" + name, list(shape), dt, kind="ExternalOutput").ap()
            dbg_d[name] = d
            P.dma("sp", lambda e, s: e.dma_start(out=d, in_=ap).then_inc(s, 16), hlist, [], "dbg_" + name)

        for sq in range(NB):
            if stage == 'S':
                break
            with ExitStack() as ph:
                L = dict(i=0, ss=[sb([128, 4], F32, "ss", ph) for _ in range(4)], junk=sb([128, D], BF16, "junk", ph),
                         hb=[sb([128, D], BF16, "hb", ph) for _ in range(2)])
                xt = [sb([128, D], F32, "xt", ph) for _ in range(4)]
                for tt in range(NT):
                    xb = xt[tt % 4]
                    P.dma("sp", lambda e, s, xb=xb, tt=tt: e.dma_start(out=xb[:], in_=x_d[sq, tt * 128:(tt + 1) * 128, :]).then_inc(s, 16), [], [xb.h()], "xt%d" % (tt % 4))
                    norm_transpose(xb[:], xb.h(), V_GA, hT, tt, L)
                P.barrier()

            if stage == 'A':
                break
            with ExitStack() as ph:
                rings(ph, 3, 1)
                v16 = sb([128, NT, 512], BF16, "v16", ph)
                hT_all = hT.hs(range(8), range(NTB))
                for j in range(8):
                    w = loadA(sA_in[j], "qk")
                    for tb in range(NTB):
                        b = projA(w, 8, hT, tb * 512, hT.hs(range(8), [tb]))
                        evac_copy(QK[:, j, tb * 512:(tb + 1) * 512], psb[b][:], [pbh[b]], [QK.h(j, tb)])
                w = loadB(sB_v, "v")
                for tt in range(NT):
                    b = nb()

                    def f(e, b=b, tt=tt, w=w):
                        for kc in range(8):
                            i = e.matmul(psb[b][:], lhsT=hT[:, kc, tt * 128:(tt + 1) * 128], rhs=w[:, kc, :], start=(kc == 0), stop=(kc == 7))
                        return i
                    P.op("pe", f, [w.h()] + hT.hs(range(8), [tt // 4]), [pbh[b]])
                    evac_copy(v16[:, tt, :], psb[b][:], [pbh[b]], [v16.h(tt)])

                NW = 6
                e_t = [sb([128, 512], F32, "e_t", ph) for _ in range(NW)]
                sp_t = [sb([128, 512], F32, "sp_t", ph) for _ in range(NW)]
                NL = 8
                NP2 = 6
                L2 = [sb([128, 512], BF16, "L2", ph) for _ in range(NP2)]
                pair_ctr = [0]
                NP4 = 6
                L4 = [sb([128, 512], BF16, "L4", ph) for _ in range(NP4)]
                quad_ctr = [0]
                qz = [[sb([128, 512], BF16, "qz", ph) for _ in range(2)] for _ in range(2)]
                for par_ in range(2):
                    for k_ in range(2):
                        P.op("pool", lambda e, q_=qz[par_][k_]: e.memset(q_[:], 0.0), [], [qz[par_][k_].h()])
                qz_ctr = [0, 0]
                L16 = [sb([128, 512], BF16, "L16", ph) for _ in range(NL)]
                w16 = [sb([128, 512], BF16, "w16", ph) for _ in range(NW)]
                SCALE = 0.125
                tiles = []
                ob_i = 0
                for h in range(8):
                    for sbk in range(NTB):
                        t0 = sbk * 512
                        nk = (t0 + 512) // 128
                        ob = 6 + (ob_i % 2)
                        ob_i += 1
                        pairs = []
                        quads = []
                        single = None
                        qzi = qz_ctr[h % 2] % 2
                        qz_ctr[h % 2] += 1
                        for j, kc in enumerate(range(nk - 1, -1, -1)):
                            t = dict(h=h, par=h % 2, hp=h // 2, sbk=sbk, t0=t0, kc=kc, ob=ob, first=(kc == nk - 1), last=(kc == 0),
                                     diag=(kc * 128 >= t0), jd=kc - t0 // 128, w=len(tiles) % NW, l=len(tiles) % NL,
                                     pairs=list(pairs), quads=list(quads), single=single, mkpair=None, mkquad=None, qzi=qzi)
                            if j % 2 == 1:
                                pidx = pair_ctr[0] % NP2
                                pair_ctr[0] += 1
                                t["mkpair"] = (pidx, single)
                                pairs.append(pidx)
                                single = None
                                if len(pairs) == 2:
                                    qidx = quad_ctr[0] % NP4
                                    quad_ctr[0] += 1
                                    t["mkquad"] = (qidx, pairs[0], pairs[1])
                                    quads.append(qidx)
                                    pairs = []
                            else:
                                single = t["l"]
                            tiles.append(t)

                def stage1(t):
                    pr = slice(64 * t["par"], 64 * t["par"] + 64)
                    et, spt, l16 = e_t[t["w"]], sp_t[t["w"]], L16[t["l"]]
                    zb = nb()
                    t["zb"] = zb
                    kc, t0, hp, jd = t["kc"], t["t0"], t["hp"], t["jd"]
                    c0 = 128 * jd if t["diag"] else 0
                    t["c0"] = c0
                    qzb = qz[t["par"]][t["qzi"]]
                    if t["first"]:
                        P.op("act", lambda e: e.copy(out=qzb[pr, :], in_=QK[pr, hp, t0:t0 + 512]), [QK.h(hp, t["sbk"])], [qzb.h()])
                    P.op("pe", lambda e: e.matmul(psb[zb][:, c0:], lhsT=QK[:, 4 + hp, kc * 128:(kc + 1) * 128], rhs=qzb[:, c0:], start=True, stop=True),
                         [QK.h(4 + hp, kc // 4), qzb.h()], [pbh[zb]])
                    P.op("act", lambda e: e.activation(out=et[:, c0:], in_=psb[zb][:, c0:], func=AF.Exp, scale=-SCALE), [pbh[zb]], [et.h()])
                    P.op("act", lambda e: e.activation(out=spt[:, c0:], in_=et[:, c0:], func=AF.Ln, bias=1.0), [et.h()], [spt.h()])
                    if t["diag"]:
                        P.op("dve", lambda e: e.scalar_tensor_tensor(out=et[:, c0:], in0=psb[zb][:, c0:], scalar=-SCALE, in1=spt[:, c0:], op0=ALU.mult, op1=ALU.subtract),
                             [pbh[zb], spt.h()], [et.h()])
                        if c0 > 0:
                            P.op("pool", lambda e: e.memset(l16[:, 0:c0], 0.0), [], [l16.h()])
                        P.op("pool", lambda e: e.tensor_tensor(out=l16[:, c0:], in0=et[:, c0:], in1=masks4[:, jd, c0:], op=ALU.mult), [et.h()], [l16.h()])
                    else:
                        P.op("dve", lambda e: e.scalar_tensor_tensor(out=l16[:], in0=psb[zb][:], scalar=-SCALE, in1=spt[:], op0=ALU.mult, op1=ALU.subtract),
                             [pbh[zb], spt.h()], [l16.h()])
                    if t["mkpair"] is not None:
                        pidx, sidx = t["mkpair"]
                        l2, lo = L2[pidx], L16[sidx]
                        P.op("pool", lambda e: e.tensor_tensor(out=l2[:], in0=lo[:], in1=l16[:], op=ALU.add), [lo.h(), l16.h()], [l2.h()])
                        if t["mkquad"] is not None:
                            qidx, pa, pb = t["mkquad"]
                            l4, la_, lb_ = L4[qidx], L2[pa], L2[pb]
                            P.op("pool", lambda e: e.tensor_tensor(out=l4[:], in0=la_[:], in1=lb_[:], op=ALU.add), [la_.h(), lb_.h()], [l4.h()])

                def stage2(t):
                    pr = slice(64 * t["par"], 64 * t["par"] + 64)
                    et, spt, l16, w16t = e_t[t["w"]], sp_t[t["w"]], L16[t["l"]], w16[t["w"]]
                    rb = t["zb"]
                    kc, t0, hp, jd, h, ob, first = t["kc"], t["t0"], t["hp"], t["jd"], t["h"], t["ob"], t["first"]
                    prevb = [L4[qi] for qi in t["quads"]] + [L2[pi] for pi in t["pairs"]] + ([L16[t["single"]]] if t["single"] is not None else [])

                    c0 = t["c0"]

                    def f(e):
                        i = e.matmul(psb[rb][:, c0:], lhsT=stri[:], rhs=l16[:, c0:], start=True, stop=(len(prevb) == 0))
                        for j, pb_ in enumerate(prevb):
                            i = e.matmul(psb[rb][:, c0:], lhsT=ones16[:], rhs=pb_[:, c0:], start=False, stop=(j == len(prevb) - 1))
                        return i
                    P.op("pe", f, [l16.h()] + [pb_.h() for pb_ in prevb], [pbh[rb]])

                def stage3(t):
                    et, spt, w16t = e_t[t["w"]], sp_t[t["w"]], w16[t["w"]]
                    rb = t["zb"]
                    jd = t["jd"]
                    c0 = t["c0"]
                    P.op("dve", lambda e: e.tensor_tensor(out=et[:, c0:], in0=psb[rb][:, c0:], in1=spt[:, c0:], op=ALU.subtract), [pbh[rb], spt.h()], [et.h()])
                    P.op("act", lambda e: e.activation(out=w16t[:, c0:], in_=et[:, c0:], func=AF.Exp), [et.h()], [w16t.h()])
                    if t["diag"]:
                        if c0 > 0:
                            P.op("pool", lambda e: e.memset(w16t[:, 0:c0], 0.0), [], [w16t.h()])
                        P.op("pool", lambda e: e.tensor_tensor(out=w16t[:, c0:], in0=w16t[:, c0:], in1=masks4[:, jd, c0:], op=ALU.mult), [w16t.h()], [w16t.h()])

                def stage4(t):
                    pr = slice(64 * t["par"], 64 * t["par"] + 64)
                    w16t = w16[t["w"]]
                    kc, t0, hp, h, ob, first = t["kc"], t["t0"], t["hp"], t["h"], t["ob"], t["first"]
                    P.op("pe", lambda e: e.matmul(psb[ob][:], lhsT=v16[:, kc, hp * 128:(hp + 1) * 128], rhs=w16t[:], start=first, stop=t["last"]),
                         [v16.h(kc), w16t.h()], [pbh[ob]])
                    if t["last"]:
                        evac_copy(OO[pr, hp, t0:t0 + 512], psb[ob][pr, :], [pbh[ob]], [OO.h(hp, t["sbk"])])

                NTL = len(tiles)
                for i in range(NTL + 6):
                    if i < NTL:
                        stage1(tiles[i])
                    if 0 <= i - 3 < NTL:
                        stage2(tiles[i - 3])
                    if 0 <= i - 4 < NTL:
                        stage3(tiles[i - 4])
                    if 0 <= i - 6 < NTL:
                        stage4(tiles[i - 6])
                if sq == dbg_sq:
                    dbg_dump("osb", OO[:, 0:4, :], OO.hs(range(4), range(NTB)), [128, 4, S], BF16)
                    dbg_dump("qk", QK[:, :, :], QK.hs(range(8), range(NTB)), [128, 8, S], BF16)
                    dbg_dump("hT", hT[:, :, :], hT.hs(range(8), range(NTB)), [128, 8, S], BF16)
                P.barrier()

            if stage == 'B':
                break
            with ExitStack() as ph:
                rings(ph, 2, 0)
                TWXA = sb([128, S], BF16, "TWXA", ph)
                SG = sb([128, S], BF16, "SG", ph)
                Ul = [sb([128, 513], F32, "Ul", ph) for _ in range(2)]
                tmp = sb([128, 512], F32, "tmp", ph)
                xs = sb([128, 512], F32, "xs", ph)
                for li, cbw in enumerate((24, 25)):
                    w = loadA(sA_in[cbw], "lora")
                    U = Ul[li]
                    P.op("pool", lambda e, U=U: e.memset(U[:, 0:1], 0.0), [], [U.h()])
                    for tb in range(NTB):
                        b = projA(w, 8, hT, tb * 512, hT.hs(range(8), [tb]))
                        P.op("act", lambda e, U=U, b=b: e.copy(out=U[:, 1:513], in_=psb[b][:]), [pbh[b]], [U.h()])
                        mc = vecs[:, V_MU + 12 + li:V_MU + 13 + li]
                        oc = omu[:, 12 + li:13 + li]
                        P.op("dve", lambda e, U=U, oc=oc: e.tensor_scalar(out=tmp[:], in0=U[:, 1:513], scalar1=oc, scalar2=None, op0=ALU.mult), [U.h()], [tmp.h()])
                        P.op("dve", lambda e, U=U, mc=mc: e.scalar_tensor_tensor(out=xs[:], in0=U[:, 0:512], scalar=mc, in1=tmp[:], op0=ALU.mult, op1=ALU.add), [U.h(), tmp.h()], [xs.h()])
                        P.op("act", lambda e, U=U: e.copy(out=U[:, 0:1], in_=U[:, 512:513]), [U.h(), xs.h()], [U.h()])
                        sl = slice(tb * 512, (tb + 1) * 512)
                        if li == 0:
                            P.op("act", lambda e, sl=sl: e.activation(out=TWXA[0:64, sl], in_=xs[0:64, :], func=AF.Tanh), [xs.h()], [TWXA.h(tb, 0)])
                            P.op("act", lambda e, sl=sl: e.copy(out=TWXA[64:128, sl], in_=xs[64:128, :]), [xs.h()], [TWXA.h(tb, 1)])
                        else:
                            P.op("act", lambda e, sl=sl: e.activation(out=SG[:, sl], in_=xs[:], func=AF.Sigmoid), [xs.h()], [SG.h(tb)])

                carve_i = [0]
                per_row = S // 1024

                def carve512(name):
                    i = carve_i[0]
                    if i >= 8 * per_row:
                        return sb([128, 512], F32, name, ph)
                    carve_i[0] += 1
                    k, half = divmod(i, per_row)
                    ap = QK.t[:, k, half * 1024:(half + 1) * 1024].bitcast(F32)
                    return Buf(ap, "cv_%s" % name)
                Wrkv = sb([128, 3, 8, 128], BF16, "Wrkv", ph)
                W32 = sb([128, 8, 128], F32, "W32", ph)
                ones32 = sb([128, 128], F32, "ones32", ph)
                x0 = sb([128, 8], F32, "x0", ph)
                t0s = sb([128, 8], F32, "t0s", ph)
                t0b = sb([128, 64], F32, "t0b", ph)
                h0 = sb([128, 8, 64], F32, "h0", ph)
                P.op("pool", lambda e: e.memset(ones32[:], 1.0), [], [ones32.h()])
                P.dma("sp", lambda e, s_: e.dma_start(out=x0[:], in_=x0T_d[sq]).then_inc(s_, 16), [], [x0.h()], "x0")
                P.op("dve", lambda e: e.tensor_tensor(out=t0s[:], in0=x0[:], in1=x0[:], op=ALU.mult), [x0.h()], [t0s.h()])
                P.op("dve", lambda e: e.reduce_sum(out=t0s[:, 0:1], in_=t0s[:], axis=mybir.AxisListType.X), [t0s.h()], [t0s.h()])
                P.op("dve", lambda e: e.tensor_copy(out=t0b[:], in_=t0s[:, 0:1].broadcast_to([128, 64])), [t0s.h()], [t0b.h()])
                b0_ = nb()
                P.op("pe", lambda e, b0_=b0_: e.matmul(psb[b0_][:, 0:64], lhsT=ones32[:], rhs=t0b[:], start=True, stop=True), [ones32.h(), t0b.h()], [pbh[b0_]])
                P.op("act", lambda e, b0_=b0_: e.activation(out=t0s[:, 2:4], in_=psb[b0_][:, 0:2], func=AF.Sqrt, scale=1.0 / D, bias=1e-6), [pbh[b0_]], [t0s.h()])
                P.op("dve", lambda e: e.reciprocal(out=t0s[:, 4:6], in_=t0s[:, 2:4]), [t0s.h()], [t0s.h()])
                P.op("dve", lambda e: e.scalar_tensor_tensor(out=x0[:], in0=x0[:], scalar=t0s[:, 4:5], in1=vecs[:, V_GA:V_GA + 8], op0=ALU.mult, op1=ALU.mult), [x0.h(), t0s.h()], [x0.h()])
                P.op("dve", lambda e: e.tensor_copy(out=h0[:], in_=x0[:].unsqueeze(2).broadcast_to([128, 8, 64])), [x0.h()], [h0.h()])
                Ur = [sb([128, 513], F32, "Ur", ph) for _ in range(3)]
                Xr = [sb([128, 512], F32, "Xr", ph) for _ in range(3)]
                lw = carve512("lw")
                At = carve512("At")
                Gt = carve512("Gt")
                kk = carve512("kk")
                kk2 = carve512("kk2")
                sd = carve512("sd")
                kkn = carve512("kkn")
                keff = carve512("keff")
                bvec = carve512("bvec")
                clw = carve512("clw")
                E1 = carve512("E1")
                E2 = carve512("E2")
                E3 = carve512("E3")
                dd = carve512("dd")
                rk = sb([128, 512], F32, "rk", ph)
                WC = sb([128, 8], F32, "WC", ph)
                AR = sb([128, 8, 128], BF16, "AR", ph)
                BK = sb([128, 8, 128], BF16, "BK", ph)
                AR32 = sb([128, 8, 128], F32, "AR32", ph)
                BK32 = sb([128, 8, 128], F32, "BK32", ph)
                BH = sb([128, 8, 64], BF16, "BH", ph)
                KH = sb([128, 8, 64], BF16, "KH", ph)
                V16 = sb([128, 512], BF16, "V16", ph)
                TOKB = sb([128, 8, 128], BF16, "TOKB", ph)
                Gs = sb([128, 2, 8, 128], BF16, "Gs", ph)
                NnA = [sb([64, 8, 64], BF16, "Nn", ph) for _ in range(2)]
                PpA = [sb([64, 8, 64], BF16, "Pp", ph) for _ in range(2)]
                XxA = [sb([64, 8, 64], BF16, "Xx", ph) for _ in range(2)]
                NnB = [sb([64, 8, 64], BF16, "Nn", ph) for _ in range(2)]
                PpB = [sb([64, 8, 64], BF16, "Pp", ph) for _ in range(2)]
                XxB = [sb([64, 8, 64], BF16, "Xx", ph) for _ in range(2)]
                TT = sb([64, 2, 8, 64], BF16, "TT", ph)
                S32 = sb([128, 64], F32, "S32", ph)
                S16z = [sb([128, 2, 64], BF16, "S16z", ph) for _ in range(2)]
                sz_i = [0]
                BDm = sb([128, 2, 64], F32, "BDm", ph)
                P.op("pool", lambda e: e.memset(BDm[:], 0.0), [], [BDm.h()])
                P.op("pool", lambda e: e.memset(BDm[0:64, 0, :], 1.0), [], [BDm.h()])
                P.op("pool", lambda e: e.memset(BDm[64:128, 1, :], 1.0), [], [BDm.h()])
                ATOK = sb([64, 8, 128], BF16, "ATOK", ph)
                Qs = sb([64, 2, 8, 64], BF16, "Qs", ph)
                W00s = sb([64, 2, 8, 64], BF16, "W00s", ph)
                MT = sb([128, 8, 64], BF16, "MT", ph)
                RP = sb([128, 8, 64], BF16, "RP", ph)
                V0 = sb([128, 8, 128], BF16, "V0", ph)
                UV = sb([128, 8, 128], BF16, "UV", ph)
                Yt, Y2, mt, msq, var, yc = kk, kk2, sd, dd, E2, E3

                def v3(t):
                    return t[:].rearrange("p (c t) -> p c t", t=64)

                for hp in range(4):
                    for idx in range(3):
                        cbw = 12 + idx * 4 + hp
                        P.dma("sp", lambda e, s, idx=idx, cbw=cbw: e.dma_start(out=Wrkv[:, idx, :, :], in_=sA_in[cbw]).then_inc(s, 16), [grp["rkv"]], [Wrkv.h(idx)], "Wrkv%d" % idx)
                        P.op("pool", lambda e, idx=idx: e.memset(Ur[idx][:, 0:1], 0.0), [], [Ur[idx].h()])
                    P.op("pool", lambda e: e.memset(S32[:], 0.0), [], [S32.h()])
                    P.op("pool", lambda e, z_=S16z[sz_i[0] % 2]: e.memset(z_[:], 0.0), [], [S16z[sz_i[0] % 2].h()])
                    if hp == 0:
                        P.op("pool", lambda e: e.memset(V0[0:64].rearrange("p c k -> p (c k)"), 0.0), [], [V0.h()])
                    def rw_block(hp, tb):
                        T0 = tb * 512
                        hTh = hT.hs(range(8), [tb])
                        for idx in range(3):
                            b = nb()

                            def f(e, b=b, idx=idx, T0=T0):
                                for kc in range(8):
                                    i = e.matmul(psb[b][:], lhsT=Wrkv[:, idx, kc, :], rhs=hT[:, kc, T0:T0 + 512], start=(kc == 0), stop=(kc == 7))
                                return i
                            P.op("pe", f, [Wrkv.h(idx)] + hTh, [pbh[b]])
                            U = Ur[idx]
                            X = Xr[idx]
                            ci = idx * 4 + hp
                            mc = vecs[:, V_MU + ci:V_MU + ci + 1]
                            oc = omu[:, ci:ci + 1]
                            P.op("act", lambda e, U=U, b=b: e.copy(out=U[:, 1:513], in_=psb[b][:]), [pbh[b]], [U.h()])
                            if tb == 0 and idx < 2:
                                cbw0 = 12 + idx * 4 + hp
                                P.dma("sp", lambda e, s_, cbw0=cbw0: e.dma_start(out=W32[:], in_=wA_in[cbw0]).then_inc(s_, 16), [], [W32.h()], "W32")
                                b2 = nb()

                                def f0(e, b2=b2):
                                    for kc in range(8):
                                        i = e.matmul(psb[b2][:, 0:64], lhsT=W32[:, kc, :], rhs=h0[:, kc, :], start=(kc == 0), stop=(kc == 7))
                                    return i
                                P.op("pe", f0, [W32.h(), h0.h()], [pbh[b2]])
                                P.op("act", lambda e, U=U, b2=b2: e.copy(out=U[:, 1:2], in_=psb[b2][:, 0:1]), [pbh[b2]], [U.h()])
                            P.op("dve", lambda e, U=U, oc=oc: e.tensor_scalar(out=tmp[:], in0=U[:, 1:513], scalar1=oc, scalar2=None, op0=ALU.mult), [U.h()], [tmp.h()])
                            P.op("dve", lambda e, U=U, mc=mc, X=X: e.scalar_tensor_tensor(out=X[:], in0=U[:, 0:512], scalar=mc, in1=tmp[:], op0=ALU.mult, op1=ALU.add), [U.h(), tmp.h()], [X.h()])
                            P.op("act", lambda e, U=U: e.copy(out=U[:, 0:1], in_=U[:, 512:513]), [U.h(), X.h()], [U.h()])
                        R, K, V = Xr
                        cs = slice(hp * 128, (hp + 1) * 128)
                        ts = slice(T0, T0 + 512)
                        b = nb()
                        P.op("pe", lambda e, b=b, cs=cs, ts=ts: e.matmul(psb[b][:], lhsT=w2a2[0:64, cs], rhs=TWXA[0:64, ts], start=True, stop=True), [TWXA.h(tb, 0)], [pbh[b]])
                        P.op("act", lambda e, b=b: e.activation(out=lw[:], in_=psb[b][:], func=AF.Sigmoid, bias=vecs[:, V_W0 + hp:V_W0 + hp + 1]), [pbh[b]], [lw.h()])
                        P.op("dve", lambda e: e.tensor_scalar(out=lw[:], in0=lw[:], scalar1=-0.6065306597126334, scalar2=None, op0=ALU.mult), [lw.h()], [lw.h()])
                        b = nb()
                        P.op("pe", lambda e, b=b, cs=cs, ts=ts: e.matmul(psb[b][:], lhsT=w2a2[64:128, cs], rhs=TWXA[64:128, ts], start=True, stop=True), [TWXA.h(tb, 1)], [pbh[b]])
                        P.op("act", lambda e, b=b: e.activation(out=At[:], in_=psb[b][:], func=AF.Sigmoid, bias=vecs[:, V_A0 + hp:V_A0 + hp + 1]), [pbh[b]], [At.h()])
                        b = nb()
                        P.op("pe", lambda e, b=b, cs=cs, ts=ts: e.matmul(psb[b][:], lhsT=g2[:, cs], rhs=SG[:, ts], start=True, stop=True), [SG.h(tb)], [pbh[b]])
                        P.op("act", lambda e, b=b: e.copy(out=Gt[:], in_=psb[b][:]), [pbh[b]], [Gt.h()])
                        P.op("dve", lambda e: e.tensor_scalar(out=kk[:], in0=K[:], scalar1=vecs[:, V_KK + hp:V_KK + hp + 1], scalar2=None, op0=ALU.mult), [K.h()], [kk.h()])
                        P.op("dve", lambda e: e.tensor_tensor(out=kk2[:], in0=kk[:], in1=kk[:], op=ALU.mult), [kk.h()], [kk2.h()])
                        b = nb()
                        P.op("pe", lambda e, b=b: e.matmul(psb[b][:], lhsT=onesbd[:], rhs=kk2[:], start=True, stop=True), [kk2.h()], [pbh[b]])
                        P.op("act", lambda e, b=b: e.activation(out=sd[:], in_=psb[b][:], func=AF.Sqrt), [pbh[b]], [sd.h()])
                        P.op("dve", lambda e: e.tensor_scalar(out=tmp[:], in0=At[:], scalar1=vecs[:, V_KA + hp:V_KA + hp + 1], scalar2=oka[:, hp:hp + 1], op0=ALU.mult, op1=ALU.add), [At.h()], [tmp.h()])
                        P.op("dve", lambda e: e.tensor_tensor(out=keff[:], in0=K[:], in1=tmp[:], op=ALU.mult), [K.h(), tmp.h()], [keff.h()])
                        P.op("dve", lambda e: e.tensor_tensor_scan(out=clw[:], data0=keepm[:], data1=lw[:], initial=0.0, op0=ALU.mult, op1=ALU.add), [lw.h()], [clw.h()])
                        P.op("dve", lambda e: e.tensor_tensor(out=dd[:], in0=clw[:], in1=lw[:], op=ALU.subtract), [clw.h(), lw.h()], [dd.h()])
                        P.op("act", lambda e: e.activation(out=E1[:], in_=clw[:], func=AF.Exp), [clw.h()], [E1.h()])
                        P.op("act", lambda e: e.activation(out=E2[:], in_=dd[:], func=AF.Exp), [dd.h()], [E2.h()])
                        P.op("act", lambda e: e.activation(out=E3[:], in_=clw[:], func=AF.Exp, scale=-1.0), [clw.h()], [E3.h()])
                        P.op("act", lambda e: e.copy(out=WC[:], in_=v3(E1)[:, :, 63]), [E1.h()], [WC.h()])
                        P.op("act", lambda e: e.copy(out=V16[:], in_=V[:]), [V.h()], [V16.h()])
                        P.op("dve", lambda e: e.tensor_tensor(out=AR32[:, :, 64:128], in0=v3(R), in1=v3(E1), op=ALU.mult), [R.h(), E1.h()], [AR32.h()])
                        P.op("dve", lambda e: e.scalar_tensor_tensor(out=rk[:], in0=R[:], scalar=vecs[:, V_RK + hp:V_RK + hp + 1], in1=keff[:], op0=ALU.mult, op1=ALU.mult), [R.h(), keff.h()], [rk.h()])
                        P.op("dve", lambda e: e.tensor_tensor(out=BK32[:, :, 64:128], in0=v3(keff), in1=v3(E3), op=ALU.mult), [keff.h(), E3.h()], [BK32.h()])
                        wcb = WC[:].unsqueeze(2).broadcast_to([128, 8, 64])
                        P.op("dve", lambda e, wcb=wcb: e.tensor_tensor(out=KH[:], in0=BK32[:, :, 64:128], in1=wcb, op=ALU.mult), [BK32.h(), WC.h()], [KH.h()])
                        P.op("dve", lambda e: e.tensor_scalar(out=sd[:], in0=sd[:], scalar1=1e-12, scalar2=None, op0=ALU.max), [sd.h()], [sd.h()])
                        P.op("dve", lambda e: e.reciprocal(out=sd[:], in_=sd[:]), [sd.h()], [sd.h()])
                        P.op("dve", lambda e: e.tensor_tensor(out=kkn[:], in0=kk[:], in1=sd[:], op=ALU.mult), [kk.h(), sd.h()], [kkn.h()])
                        P.op("dve", lambda e: e.tensor_tensor(out=bvec[:], in0=kkn[:], in1=At[:], op=ALU.mult), [kkn.h(), At.h()], [bvec.h()])
                        P.op("dve", lambda e: e.scalar_tensor_tensor(out=AR32[:, :, 0:64], in0=v3(kkn), scalar=-1.0, in1=v3(E2), op0=ALU.mult, op1=ALU.mult), [kkn.h(), E2.h()], [AR32.h()])
                        P.op("act", lambda e: e.copy(out=AR[:], in_=AR32[:]), [AR32.h()], [AR.h()])
                        P.op("dve", lambda e: e.tensor_tensor(out=BK32[:, :, 0:64], in0=v3(bvec), in1=v3(E3), op=ALU.mult), [bvec.h(), E3.h()], [BK32.h()])
                        P.op("act", lambda e: e.copy(out=BK[:], in_=BK32[:]), [BK32.h()], [BK.h()])
                        P.op("dve", lambda e, wcb=wcb: e.tensor_tensor(out=BH[:], in0=BK32[:, :, 0:64], in1=wcb, op=ALU.mult), [BK32.h(), WC.h()], [BH.h()])
                        bx = nb()
                        by = nb()
                        pvx = psb[bx][:].bitcast(BF16)
                        pvy = psb[by][:].bitcast(BF16)

                        def ftr(e, pvx=pvx, pvy=pvy):
                            for c in range(8):
                                e.transpose(pvx[0:64, c * 128:(c + 1) * 128], BH[:, c, :], ident[:])
                                e.transpose(pvx[64:128, c * 128:(c + 1) * 128], KH[:, c, :], ident[:])
                                i = e.transpose(pvy[64:128, c * 128:(c + 1) * 128], V16[:, c * 64:(c + 1) * 64], ident[:])
                            return i
                        P.op("pe", ftr, [BH.h(), KH.h(), V16.h()], [pbh[bx], pbh[by]])
                        P.op("act", lambda e, pvx=pvx: e.copy(out=TOKB[:].rearrange("p c k -> p (c k)"), in_=pvx), [pbh[bx]], [TOKB.h()])
                        P.op("dve", lambda e, pvy=pvy: e.tensor_copy(out=V0[64:128].rearrange("p c k -> p (c k)"), in_=pvy[64:128, :]), [pbh[by]], [V0.h()])
                        P.op("dve", lambda e, pvy=pvy: e.tensor_copy(out=UV[64:128].rearrange("p c k -> p (c k)"), in_=pvy[64:128, :]), [pbh[by]], [UV.h()])
                        ba = nb()
                        pva = psb[ba][:].bitcast(BF16)

                        def fta(e, pva=pva):
                            for c in range(8):
                                i = e.transpose(pva[0:64, c * 128:(c + 1) * 128], AR[:, c, 0:64], ident[:])
                            return i
                        P.op("pe", fta, [AR.h()], [pbh[ba]])
                        P.op("act", lambda e, pva=pva: e.copy(out=ATOK[:].rearrange("p c k -> p (c k)"), in_=pva[0:64, :]), [pbh[ba]], [ATOK.h()])
                        def inv_gen(par, Nn, Pp, Xx):
                            pr = slice(64 * par, 64 * par + 64)
                            for g4 in range(2):
                                b = nb()

                                def fg(e, b=b, g4=g4, pr=pr):
                                    for cc in range(4):
                                        c = g4 * 4 + cc
                                        i = e.matmul(psb[b][:, cc * 128:(cc + 1) * 128], lhsT=BK32[pr, c, :], rhs=AR32[pr, c, :], start=True, stop=True)
                                    return i
                                P.op("pe", fg, [BK32.h(), AR32.h()], [pbh[b]])
                                mgb = MG[:].unsqueeze(1).broadcast_to([128, 4, 128])
                                P.op("dve", lambda e, b=b, g4=g4, par=par, mgb=mgb: e.tensor_tensor(out=Gs[:, par, g4 * 4:(g4 + 1) * 4, :], in0=psb[b][:].rearrange("p (c k) -> p c k", k=128), in1=mgb, op=ALU.mult),
                                     [pbh[b]], [Gs.h(par)])
                            b = nb()

                            def fn_(e, b=b, pr=pr):
                                for c in range(8):
                                    i = e.matmul(psb[b][0:64, c * 64:(c + 1) * 64], lhsT=AR[pr, c, 0:64], rhs=BK[pr, c, 0:64], start=True, stop=True)
                                return i
                            P.op("pe", fn_, [AR.h(), BK.h()], [pbh[b]])
                            mnb = MN[:].unsqueeze(1).broadcast_to([64, 8, 64])
                            P.op("dve", lambda e, b=b, mnb=mnb: e.tensor_tensor(out=Nn[0][:], in0=psb[b][0:64, :].rearrange("p (c k) -> p c k", k=64), in1=mnb, op=ALU.mult), [pbh[b]], [Nn[0].h()])
                            yield
                            bw = nb()

                            def fw00(e, b=bw, par=par):
                                for c in range(8):
                                    i = e.matmul(psb[b][0:64, c * 64:(c + 1) * 64], lhsT=Gs[:, par, c, 0:64], rhs=V0[:, c, par * 64:(par + 1) * 64], start=True, stop=True)
                                return i
                            P.op("pe", fw00, [Gs.h(par), V0.h()], [pbh[bw]])
                            P.op("dve", lambda e, b=bw, par=par: e.tensor_copy(out=W00s[:, par, :, :], in_=psb[b][0:64, :].rearrange("p (c k) -> p c k", k=64)), [pbh[bw]], [W00s.h()])
                            P.op("act", lambda e, par=par: e.copy(out=Pp[0][:], in_=Gs[0:64, par, :, 0:64]), [Gs.h(par)], [Pp[0].h()])
                            idb = ident[0:64, 0:64].unsqueeze(1).broadcast_to([64, 8, 64])
                            P.op("pool", lambda e, idb=idb: e.tensor_tensor(out=Xx[0][:], in0=Pp[0][:], in1=idb, op=ALU.add), [Pp[0].h()], [Xx[0].h()])
                            yield
                            cur = 0
                            for lv in range(1, 6):
                                nx = 1 - cur
                                Pc, Nc, Xc = Pp[cur], Nn[cur], Xx[cur]
                                Pn, Nx, Xn = Pp[nx], Nn[nx], Xx[nx]
                                if lv <= 4:
                                    b = nb()

                                    def fp(e, b=b, Pc=Pc, Nc=Nc):
                                        for c in range(8):
                                            i = e.matmul(psb[b][0:64, c * 64:(c + 1) * 64], lhsT=Nc[:, c, :], rhs=Pc[:, c, :], start=True, stop=True)
                                        return i
                                    P.op("pe", fp, [Pc.h(), Nc.h()], [pbh[b]])
                                    P.op("act", lambda e, b=b, Pn=Pn: e.copy(out=Pn[:].rearrange("p c k -> p (c k)"), in_=psb[b][0:64, :]), [pbh[b]], [Pn.h()])
                                b = nb()

                                def fnn(e, b=b, Pc=Pc, Nc=Nc):
                                    for c in range(8):
                                        i = e.matmul(psb[b][0:64, c * 64:(c + 1) * 64], lhsT=Pc[:, c, :], rhs=Nc[:, c, :], start=True, stop=True)
                                    return i
                                P.op("pe", fnn, [Pc.h(), Nc.h()], [pbh[b]])
                                P.op("dve", lambda e, b=b, Nx=Nx: e.tensor_copy(out=Nx[:].rearrange("p c k -> p (c k)"), in_=psb[b][0:64, :]), [pbh[b]], [Nx.h()])
                                yield
                                b = nb()

                                def fx(e, b=b, Nx=Nx, Xc=Xc):
                                    for c in range(8):
                                        e.matmul(psb[b][0:64, c * 64:(c + 1) * 64], lhsT=Nx[:, c, :], rhs=Xc[:, c, :], start=True, stop=False)
                                        i = e.matmul(psb[b][0:64, c * 64:(c + 1) * 64], lhsT=ident[0:64, 0:64], rhs=Xc[:, c, :], start=False, stop=True)
                                    return i
                                P.op("pe", fx, [Nx.h(), Xc.h()], [pbh[b]])
                                if lv < 5:
                                    P.op("act", lambda e, b=b, Xn=Xn: e.copy(out=Xn[:].rearrange("p c k -> p (c k)"), in_=psb[b][0:64, :]), [pbh[b]], [Xn.h()])
                                else:
                                    P.op("act", lambda e, b=b, par=par: e.copy(out=TT[:, par, :, :], in_=psb[b][0:64, :].rearrange("p (c k) -> p c k", k=64)), [pbh[b]], [TT.h(par)])
                                cur = nx
                                yield
                        gens = [inv_gen(0, NnA, PpA, XxA), inv_gen(1, NnB, PpB, XxB)]
                        while gens:
                            for g_ in list(gens):
                                try:
                                    next(g_)
                                except StopIteration:
                                    gens.remove(g_)
                        bq = [nb(), nb()]
                        for par in range(2):
                            def fq(e, par=par, b=bq[par]):
                                for c in range(8):
                                    i = e.matmul(psb[b][0:64, c * 64:(c + 1) * 64], lhsT=TT[:, par, c, :], rhs=ATOK[:, c, par * 64:(par + 1) * 64], start=True, stop=True)
                                return i
                            P.op("pe", fq, [TT.h(par), ATOK.h()], [pbh[bq[par]]])
                        bu = [nb(), nb()]
                        for par in range(2):
                            def fu0(e, par=par, b=bu[par]):
                                for c in range(8):
                                    i = e.matmul(psb[b][0:64, c * 64:(c + 1) * 64], lhsT=TT[:, par, c, :], rhs=W00s[:, par, c, :], start=True, stop=True)
                                return i
                            P.op("pe", fu0, [TT.h(par), W00s.h()], [pbh[bu[par]]])
                        P.op("act", lambda e, b=bq[0]: e.copy(out=Qs[:, 0, :, :], in_=psb[b][0:64, :].rearrange("p (c k) -> p c k", k=64)), [pbh[bq[0]]], [Qs.h(0)])
                        P.op("dve", lambda e, b=bq[1]: e.tensor_copy(out=Qs[:, 1, :, :], in_=psb[b][0:64, :].rearrange("p (c k) -> p c k", k=64)), [pbh[bq[1]]], [Qs.h(1)])
                        P.op("act", lambda e, b=bu[0]: e.copy(out=UV[0:64, :, 0:64], in_=psb[b][0:64, :].rearrange("p (c k) -> p c k", k=64)), [pbh[bu[0]]], [UV.h()])
                        P.op("dve", lambda e, b=bu[1]: e.tensor_copy(out=UV[0:64, :, 64:128], in_=psb[b][0:64, :].rearrange("p (c k) -> p c k", k=64)), [pbh[bu[1]]], [UV.h()])
                        bm = nb()

                        def fm(e, b=bm):
                            for par in range(2):
                                pr = slice(64 * par, 64 * par + 64)
                                for c in range(8):
                                    i = e.matmul(psb[b][pr, c * 64:(c + 1) * 64], lhsT=Qs[:, par, c, :], rhs=TOKB[0:64, c, par * 64:(par + 1) * 64], start=True, stop=True)
                            return i
                        P.op("pe", fm, [Qs.h(0), Qs.h(1), TOKB.h()], [pbh[bm]])
                        br = nb()

                        def frp(e, b=br):
                            for par in range(2):
                                pr = slice(64 * par, 64 * par + 64)
                                for c in range(8):
                                    i = e.matmul(psb[b][pr, c * 64:(c + 1) * 64], lhsT=Qs[:, par, c, :], rhs=Gs[0:64, par, c, 64:128], start=True, stop=True)
                            return i
                        P.op("pe", frp, [Qs.h(0), Qs.h(1), Gs.h(0), Gs.h(1)], [pbh[br]])
                        P.op("act", lambda e, b=bm: e.copy(out=MT[:].rearrange("p c k -> p (c k)"), in_=psb[b][:]), [pbh[bm]], [MT.h()])
                        P.op("dve", lambda e, b=br: e.tensor_tensor(out=RP[:], in0=psb[b][:].rearrange("p (c k) -> p c k", k=64), in1=AR32[:, :, 64:128], op=ALU.add), [pbh[br], AR32.h()], [RP.h()])
                        yb = 6
                        bdb = BDm[:]
                        for c in range(8):
                            Sc = S16z[sz_i[0] % 2]
                            Sn = S16z[(sz_i[0] + 1) % 2]
                            sz_i[0] += 1
                            b = nb()

                            def frec(e, b=b, c=c, Sc=Sc):
                                for par in range(2):
                                    pr = slice(64 * par, 64 * par + 64)
                                    o = psb[b][pr, 0:64]
                                    e.matmul(o, lhsT=TOKB[:, c, par * 64:(par + 1) * 64], rhs=UV[:, c, par * 64:(par + 1) * 64], start=True, stop=False)
                                    i = e.matmul(o, lhsT=MT[:, c, :], rhs=Sc[:, par, :], start=False, stop=True)
                                return i
                            P.op("pe", frec, [TOKB.h(), UV.h(), MT.h(), Sc.h()], [pbh[b]])

                            def fy(e, c=c, Sc=Sc):
                                for par in range(2):
                                    pr = slice(64 * par, 64 * par + 64)
                                    o = psb[yb][pr, c * 64:(c + 1) * 64]
                                    e.matmul(o, lhsT=UV[:, c, par * 64:(par + 1) * 64], rhs=Gs[:, par, c, 64:128], start=True, stop=False)
                                    i = e.matmul(o, lhsT=Sc[:, par, :], rhs=RP[:, c, :], start=False, stop=True)
                                return i
                            P.op("pe", fy, [UV.h(), Gs.h(0), Gs.h(1), Sc.h(), RP.h()], [pbh[yb]])
                            P.op("dve", lambda e, b=b, c=c: e.scalar_tensor_tensor(out=S32[:], in0=S32[:], scalar=WC[:, c:c + 1], in1=psb[b][:, 0:64], op0=ALU.mult, op1=ALU.add), [S32.h(), WC.h(), pbh[b]], [S32.h()])
                            P.op("dve", lambda e, Sn=Sn, bdb=bdb: e.tensor_tensor(out=Sn[:], in0=S32[:].unsqueeze(1).broadcast_to([128, 2, 64]), in1=bdb, op=ALU.mult), [S32.h()], [Sn.h()])
                        P.op("act", lambda e: e.copy(out=Yt[:], in_=psb[yb][:]), [pbh[yb]], [Yt.h()])
                        P.op("dve", lambda e: e.tensor_tensor(out=Y2[:], in0=Yt[:], in1=Yt[:], op=ALU.mult), [Yt.h()], [Y2.h()])
                        bm = nb()
                        P.op("pe", lambda e, bm=bm: e.matmul(psb[bm][:], lhsT=onesbd[:], rhs=Yt[:], start=True, stop=True), [Yt.h()], [pbh[bm]])
                        be = nb()
                        P.op("pe", lambda e, be=be: e.matmul(psb[be][:], lhsT=onesbd[:], rhs=Y2[:], start=True, stop=True), [Y2.h()], [pbh[be]])
                        P.op("act", lambda e, bm=bm: e.activation(out=mt[:], in_=psb[bm][:], func=AF.Identity, scale=1.0 / 64), [pbh[bm]], [mt.h()])
                        P.op("dve", lambda e: e.tensor_tensor(out=msq[:], in0=mt[:], in1=mt[:], op=ALU.mult), [mt.h()], [msq.h()])
                        P.op("dve", lambda e, be=be: e.scalar_tensor_tensor(out=var[:], in0=psb[be][:], scalar=1.0 / 64, in1=msq[:], op0=ALU.mult, op1=ALU.subtract), [pbh[be], msq.h()], [var.h()])
                        P.op("act", lambda e: e.activation(out=var[:], in_=var[:], func=AF.Sqrt, bias=64e-5), [var.h()], [var.h()])
                        P.op("dve", lambda e: e.reciprocal(out=var[:], in_=var[:]), [var.h()], [var.h()])
                        P.op("dve", lambda e: e.tensor_tensor(out=yc[:], in0=Yt[:], in1=mt[:], op=ALU.subtract), [Yt.h(), mt.h()], [yc.h()])
                        P.op("dve", lambda e: e.tensor_tensor(out=yc[:], in0=yc[:], in1=var[:], op=ALU.mult), [yc.h(), var.h()], [yc.h()])
                        P.op("dve", lambda e: e.tensor_scalar(out=yc[:], in0=yc[:], scalar1=vecs[:, V_LNW + hp:V_LNW + hp + 1], scalar2=vecs[:, V_LNB + hp:V_LNB + hp + 1], op0=ALU.mult, op1=ALU.add), [yc.h()], [yc.h()])
                        bb = nb()
                        P.op("pe", lambda e, bb=bb: e.matmul(psb[bb][:], lhsT=onesbd[:], rhs=rk[:], start=True, stop=True), [rk.h()], [pbh[bb]])
                        P.op("dve", lambda e, bb=bb: e.tensor_tensor(out=tmp[:], in0=psb[bb][:], in1=V[:], op=ALU.mult), [pbh[bb], V.h()], [tmp.h()])
                        P.op("dve", lambda e: e.tensor_tensor(out=yc[:], in0=yc[:], in1=tmp[:], op=ALU.add), [yc.h(), tmp.h()], [yc.h()])
                        P.op("dve", lambda e, ts=ts: e.tensor_tensor(out=OO[:, 4 + hp, ts], in0=yc[:], in1=Gt[:], op=ALU.mult), [yc.h(), Gt.h()], [OO.h(4 + hp, tb)])
                    for tb in range(NTB):
                        rw_block(hp, tb)
                if sq == dbg_sq:
                    dbg_dump("orw", OO[:, 4:8, :], OO.hs(range(4, 8), range(NTB)), [128, 4, S], BF16)
                P.barrier()

            if stage == 'C':
                break
            ox = ExitStack()
            ox.__enter__()
            x2_pre = sb([128, NTH, D], F32, "x2", ox)
            with ExitStack() as ph:
                rings(ph, 8, 0)
                sg_t = [sb([128, 512], F32, "sg_t", ph) for _ in range(2)]
                m_t = [sb([128, 512], F32, "m_t", ph) for _ in range(2)]
                for cb in range(8):
                    wgs = loadA(sA_in[26 + cb], "gates")
                    wgr = loadA(sA_in[34 + cb], "gates")
                    wus = loadA(sA_us[cb], "us", 4)
                    wur = loadA(sA_ur[cb], "ur", 4)
                    for tb in range(NTB):
                        T0 = tb * 512
                        hTh = hT.hs(range(8), [tb])
                        bgs = projA(wgs, 8, hT, T0, hTh)
                        P.op("act", lambda e, bgs=bgs: e.activation(out=sg_t[0][:], in_=psb[bgs][:], func=AF.Sigmoid), [pbh[bgs]], [sg_t[0].h()])
                        bgr = projA(wgr, 8, hT, T0, hTh)
                        P.op("act", lambda e, bgr=bgr: e.activation(out=sg_t[1][:], in_=psb[bgr][:], func=AF.Sigmoid), [pbh[bgr]], [sg_t[1].h()])
                        bus = nb()

                        def f1(e, bus=bus, wus=wus, T0=T0):
                            for kc in range(4):
                                i = e.matmul(psb[bus][:], lhsT=wus[:, kc, :], rhs=OO[:, kc, T0:T0 + 512], start=(kc == 0), stop=(kc == 3))
                            return i
                        P.op("pe", f1, [wus.h()] + OO.hs(range(4), [tb]), [pbh[bus]])
                        P.op("dve", lambda e, bus=bus: e.tensor_tensor(out=m_t[0][:], in0=psb[bus][:], in1=sg_t[0][:], op=ALU.mult), [pbh[bus], sg_t[0].h()], [m_t[0].h()])
                        bur = nb()

                        def f2(e, bur=bur, wur=wur, T0=T0):
                            for kc in range(4):
                                i = e.matmul(psb[bur][:], lhsT=wur[:, kc, :], rhs=OO[:, 4 + kc, T0:T0 + 512], start=(kc == 0), stop=(kc == 3))
                            return i
                        P.op("pe", f2, [wur.h()] + OO.hs(range(4, 8), [tb]), [pbh[bur]])
                        P.op("dve", lambda e, bur=bur: e.tensor_tensor(out=m_t[1][:], in0=psb[bur][:], in1=sg_t[1][:], op=ALU.mult), [pbh[bur], sg_t[1].h()], [m_t[1].h()])
                        P.op("pool", lambda e, cb=cb, T0=T0: e.tensor_tensor(out=QK[:, cb, T0:T0 + 512], in0=m_t[0][:], in1=m_t[1][:], op=ALU.add), [m_t[0].h(), m_t[1].h()], [QK.h(cb, tb)])
                for tt in range(NTH):
                    P.dma("sp", lambda e, s, tt=tt: e.dma_start(out=x2_pre[:, tt, :], in_=x_d[sq, tt * 128:(tt + 1) * 128, :]).then_inc(s, 16), [], [x2_pre.h(tt)], "x2_%d" % tt)
                P.barrier()

            if stage == 'D1':
                break
            for hf in range(2):
                H0 = hf * HALF
                with ExitStack() as ph:
                    rings(ph, 4, 3)
                    x2 = x2_pre if hf == 0 else sb([128, NTH, D], F32, "x2", ph)
                    L = dict(i=0, ss=[sb([128, 4], F32, "ss", ph) for _ in range(4)], junk=sb([128, D], BF16, "junk", ph),
                             hb=[sb([128, D], BF16, "hb", ph) for _ in range(2)])
                    for tt in (range(NTH) if hf == 1 else []):
                        P.dma("sp", lambda e, s, tt=tt: e.dma_start(out=x2[:, tt, :], in_=x_d[sq, H0 + tt * 128:H0 + (tt + 1) * 128, :]).then_inc(s, 16), [], [x2.h(tt)], "x2_%d" % tt)
                    for cbo in range(2):
                        w = loadB(sB_out[cbo], "out")
                        for tt in range(NTH):
                            b = nb()
                            tg = H0 + tt * 128

                            def f(e, b=b, w=w, tg=tg):
                                for kc in range(8):
                                    i = e.matmul(psb[b][:], lhsT=QK[:, kc, tg:tg + 128], rhs=w[:, kc, :], start=(kc == 0), stop=(kc == 7))
                                return i
                            P.op("pe", f, [w.h()] + QK.hs(range(8), [tg // 512]), [pbh[b]])
                            o = x2[:, tt, cbo * 512:(cbo + 1) * 512]
                            P.op("dve", lambda e, b=b, o=o: e.tensor_tensor(out=o, in0=psb[b][:], in1=o, op=ALU.add), [pbh[b], x2.h(tt)], [x2.h(tt)])
                    for tt in range(NTH):
                        norm_transpose(x2[:, tt, :], x2.h(tt), V_GM, hT, (H0 // 128) + tt, L)
                    relu_t = [sb([128, 512], F32, "relu_t", ph) for _ in range(2)]
                    rl_i = [0]
                    for g in range(4):
                        aT = OO
                        A0 = (g % 2) * HALF
                        for c8 in range(8):
                            w = loadA(sA_ff1[g * 8 + c8], "ff1_%d" % g)
                            for tb in range(NTBH):
                                b = projA(w, 8, hT, H0 + tb * 512, hT.hs(range(8), [(H0 + tb * 512) // 512]))
                                o = OO[:, c8, A0 + tb * 512:A0 + (tb + 1) * 512]
                                wr = [OO.h(c8, (A0 + tb * 512) // 512)]
                                rl = relu_t[rl_i[0] % 2]
                                rl_i[0] += 1
                                P.op("act", lambda e, b=b, rl=rl: e.activation(out=rl[:], in_=psb[b][:], func=AF.Relu), [pbh[b]], [rl.h()])
                                P.op("pool", lambda e, o=o, rl=rl: e.tensor_tensor(out=o, in0=rl[:], in1=rl[:], op=ALU.mult), [rl.h()], wr)
                        for cbo in range(2):
                            w = loadB(sB_ff2[g, cbo], "ff2_%d" % g)
                            for tt in range(NTH):
                                b = nb()
                                ta = A0 + tt * 128

                                def f(e, b=b, w=w, ta=ta):
                                    for kc in range(8):
                                        i = e.matmul(psb[b][:], lhsT=OO[:, kc, ta:ta + 128], rhs=w[:, kc, :], start=(kc == 0), stop=(kc == 7))
                                    return i
                                P.op("pe", f, [w.h()] + OO.hs(range(8), [ta // 512]), [pbh[b]])
                                o = x2[:, tt, cbo * 512:(cbo + 1) * 512]
                                P.op("dve", lambda e, b=b, o=o: e.tensor_tensor(out=o, in0=psb[b][:], in1=o, op=ALU.add), [pbh[b], x2.h(tt)], [x2.h(tt)])
                    for tt in range(NTH):
                        norm_transpose(x2[:, tt, :], x2.h(tt), V_GP, hT, (H0 // 128) + tt, L)
                    pt = [sb([128, 256], F32, "pt", ph) for _ in range(2)]
                    pb16 = [sb([128, 256], BF16, "pb16", ph) for _ in range(2)]
                    pT = sb([128, 2, HALF], BF16, "pT", ph)
                    for tt in range(NTH):
                        ptt = pt[tt % 2]
                        pbt = pb16[tt % 2]
                        P.dma("sp", lambda e, s, ptt=ptt, tt=tt: e.dma_start(out=ptt[:], in_=p_d[sq, H0 + tt * 128:H0 + (tt + 1) * 128, :]).then_inc(s, 16), [], [ptt.h()], "pt%d" % (tt % 2))
                        P.op("pool", lambda e, ptt=ptt, pbt=pbt: e.tensor_copy(out=pbt[:], in_=ptt[:]), [ptt.h()], [pbt.h()])
                        b = nb()
                        pv = psb[b][:].bitcast(BF16)

                        def ftp(e, pv=pv, pbt=pbt):
                            e.transpose(pv[:, 0:128], pbt[:, 0:128], ident[:])
                            return e.transpose(pv[:, 128:256], pbt[:, 128:256], ident[:])
                        P.op("pe", ftp, [pbt.h()], [pbh[b]])
                        P.op("act", lambda e, pv=pv, tt=tt: e.copy(out=pT[:, :, tt * 128:(tt + 1) * 128], in_=pv[:, 0:256].rearrange("p (k t) -> p k t", t=128)), [pbh[b]], [pT.h(tt)])
                    sgp = [sb([128, 512], F32, "sgp", ph) for _ in range(2)]
                    for cbo in range(2):
                        wg = loadB(sB_pg[cbo], "pg")
                        wp = loadB(sB_pp[cbo], "pp", 2)
                        for tt in range(NTH):
                            tg = H0 + tt * 128
                            bg = nb()

                            def f(e, bg=bg, wg=wg, tg=tg):
                                for kc in range(8):
                                    i = e.matmul(psb[bg][:], lhsT=hT[:, kc, tg:tg + 128], rhs=wg[:, kc, :], start=(kc == 0), stop=(kc == 7))
                                return i
                            P.op("pe", f, [wg.h()] + hT.hs(range(8), [tg // 512]), [pbh[bg]])
                            s_ = sgp[tt % 2]
                            P.op("act", lambda e, bg=bg, s_=s_: e.activation(out=s_[:], in_=psb[bg][:], func=AF.Sigmoid), [pbh[bg]], [s_.h()])
                            bp = nb()

                            def f2(e, bp=bp, wp=wp, tt=tt):
                                for kc in range(2):
                                    i = e.matmul(psb[bp][:], lhsT=pT[:, kc, tt * 128:(tt + 1) * 128], rhs=wp[:, kc, :], start=(kc == 0), stop=(kc == 1))
                                return i
                            P.op("pe", f2, [wp.h(), pT.h(tt)], [pbh[bp]])
                            P.op("dve", lambda e, bp=bp, s_=s_: e.tensor_tensor(out=s_[:], in0=psb[bp][:], in1=s_[:], op=ALU.mult), [pbh[bp], s_.h()], [s_.h()])
                            o = x2[:, tt, cbo * 512:(cbo + 1) * 512]
                            P.op("pool", lambda e, o=o, s_=s_: e.tensor_tensor(out=o, in0=o, in1=s_[:], op=ALU.add), [s_.h(), x2.h(tt)], [x2.h(tt)])
                    ot = [sb([128, D], F32, "ot", ph) for _ in range(2)]
                    for tt in range(NTH):
                        ss = L["ss"][L["i"] % 4]
                        L["i"] += 1
                        junk = L["junk"]
                        o_ = ot[tt % 2]
                        xa = x2[:, tt, :]
                        P.op("act", lambda e, xa=xa, ss=ss: e.activation(out=junk[:], in_=xa, func=AF.Square, accum_out=ss[:, 0:1]), [x2.h(tt)], [junk.h(), ss.h()])
                        P.op("act", lambda e, ss=ss: e.activation(out=ss[:, 1:2], in_=ss[:, 0:1], func=AF.Sqrt, scale=1.0 / D, bias=1e-6), [ss.h()], [ss.h()])
                        P.op("dve", lambda e, ss=ss: e.reciprocal(out=ss[:, 2:3], in_=ss[:, 1:2]), [ss.h()], [ss.h()])
                        P.op("dve", lambda e, xa=xa, ss=ss, o_=o_: e.scalar_tensor_tensor(out=o_[:], in0=xa, scalar=ss[:, 2:3], in1=gF[:], op0=ALU.mult, op1=ALU.mult), [x2.h(tt), ss.h()], [o_.h()])
                        P.dma("sp", lambda e, s, o_=o_, tt=tt: e.dma_start(out=out_d[sq, H0 + tt * 128:H0 + (tt + 1) * 128, :], in_=o_[:]).then_inc(s, 16), [o_.h()], [], "st%d" % (tt % 2))
                    P.barrier()
                if hf == 0:
                    ox.close()
        P.finish()
        ninst = P.ninst
    return nc, dbg_d, ninst


def prep_weights(inp):
    f = lambda a: np.ascontiguousarray(np.asarray(a, dtype=np.float32))
    w_in = f(inp["w_in"])[0]
    d = {}
    d["wA_in"] = f(w_in.reshape(8, 128, 42, 128).transpose(2, 1, 0, 3))
    d["wB_v"] = f(w_in[:, 1024:1536].reshape(8, 128, 512).transpose(1, 0, 2))
    d["wA_us"] = f(f(inp["w_up_sb"])[0].reshape(4, 128, 8, 128).transpose(2, 1, 0, 3))
    d["wA_ur"] = f(f(inp["w_up_rw"])[0].reshape(4, 128, 8, 128).transpose(2, 1, 0, 3))
    d["wB_out"] = f(f(inp["w_out"])[0].reshape(8, 128, 2, 512).transpose(2, 1, 0, 3))
    d["wA_ff1"] = f(f(inp["w_ff1"])[0].reshape(8, 128, 32, 128).transpose(2, 1, 0, 3))
    d["wB_ff2"] = f(f(inp["w_ff2"])[0].reshape(4, 8, 128, 2, 512).transpose(0, 3, 2, 1, 4))
    d["wB_pg"] = f(f(inp["w_ple_gate"])[0].reshape(8, 128, 2, 512).transpose(2, 1, 0, 3))
    d["wB_pp"] = f(f(inp["w_ple_proj"])[0].reshape(2, 128, 2, 512).transpose(2, 1, 0, 3))
    d["w2a2"] = f(np.concatenate([f(inp["decay_w2"])[0], f(inp["iclr_a2"])[0]], axis=0))
    d["g2"] = f(inp["gate_g2"])[0]
    vecs = np.zeros((128, NV), np.float32)
    col = lambda v, n: f(v).reshape(n, 128).T
    vecs[:, V_GA:V_GA + 8] = col(inp["attn_norm_g"], 8)
    vecs[:, V_GM:V_GM + 8] = col(inp["mlp_norm_g"], 8)
    vecs[:, V_GP:V_GP + 8] = col(inp["ple_norm_g"], 8)
    vecs[:, V_MU:V_MU + 14] = col(inp["shift_mu"], 14)
    vecs[:, V_W0:V_W0 + 4] = col(inp["decay_w0"], 4)
    vecs[:, V_A0:V_A0 + 4] = col(inp["iclr_a0"], 4)
    vecs[:, V_KK:V_KK + 4] = col(inp["k_k"], 4)
    vecs[:, V_KA:V_KA + 4] = col(inp["k_a"], 4)
    vecs[:, V_RK:V_RK + 4] = col(inp["r_k"], 4)
    vecs[:, V_LNW:V_LNW + 4] = col(inp["ln_x_w"], 4)
    vecs[:, V_LNB:V_LNB + 4] = col(inp["ln_x_b"], 4)
    d["vecs"] = vecs
    d["gF"] = f(np.broadcast_to(f(inp["final_norm_g"]).reshape(1, D), (128, D)))
    return d


def kernel(**inputs):
    x = np.asarray(inputs["x"], dtype=np.float32)
    p = np.asarray(inputs["p"], dtype=np.float32)[0]
    B, S, _ = x.shape
    NB = B // NCORES
    nc, _, _ = build(S, NB)
    wd = prep_weights(inputs)
    in_maps = []
    for c in range(NCORES):
        m = dict(wd)
        m["x"] = np.ascontiguousarray(x[c * NB:(c + 1) * NB])
        m["p"] = np.ascontiguousarray(p[c * NB:(c + 1) * NB])
        m["x0T"] = np.ascontiguousarray(x[c * NB:(c + 1) * NB, 0, :].reshape(NB, 8, 128).transpose(0, 2, 1))
        in_maps.append(m)
    res = run_bass_kernel_spmd(nc, in_maps, core_ids=list(range(NCORES)))
    return np.concatenate([np.asarray(r["out"], dtype=np.float32) for r in res.results], axis=0)
```
